# Optimizing a Trainium2 kernel written in Bass

```python
import jax, jax.numpy as jnp
from jax import lax
import numpy as np

D_MODEL = 2048
BATCH = 1
SEQ = 8192
DEPTH = 2

GRID_W = 64
CTX_LEN = 256
HEAD_DIM = 128
N_HEADS = D_MODEL // HEAD_DIM
NA_HEADS = N_HEADS // 2
NA_WIN_H = 8
NA_WIN_W = 16
SW_HEADS = N_HEADS - NA_HEADS
SW_KV_HEADS = 2
SW_RADIUS = 128
GQA_HEADS = N_HEADS
GQA_KV_HEADS = 4
BLOCK = 128
D_FF = ((8 * D_MODEL // 3 + 255) // 256) * 256
MACARON_WEIGHT = 0.5
ROPE_THETA = 10000.0
EPS = 1e-6
NEG_INF = -1e30
N_MOD = 9
N_EVEN = (DEPTH + 1) // 2
N_ODD = DEPTH // 2
AB_IN = (NA_HEADS + SW_HEADS + 2 * NA_HEADS + 2 * SW_KV_HEADS) * HEAD_DIM
C_IN = (GQA_HEADS + 2 * GQA_KV_HEADS) * HEAD_DIM
ATTN_SCALE = HEAD_DIM ** -0.5

kernel_name = 'hybrid_natten_swa_gqa_macaron_dit'


def rms_norm(x, g):
    xf = x.astype(jnp.float32)
    y = xf * lax.rsqrt(jnp.mean(xf * xf, axis=-1, keepdims=True) + EPS)
    return (y * g.astype(jnp.float32)).astype(x.dtype)


def modulate(x, g, shift, scale):
    return rms_norm(x, g) * (1 + scale) + shift


def heads(t, n):
    return t.reshape(t.shape[:-1] + (n, HEAD_DIM))


def swiglu(h, w_gate, w_up, w_down):
    return (jax.nn.silu(h @ w_gate) * (h @ w_up)) @ w_down


def axial_rope_tables(n_tokens):
    t = jnp.arange(n_tokens)
    row = (t // GRID_W).astype(jnp.float32)
    col = (t % GRID_W).astype(jnp.float32)
    axis_dim = HEAD_DIM // 2
    inv = ROPE_THETA ** (-jnp.arange(0, axis_dim, 2, dtype=jnp.float32) / axis_dim)
    ang = jnp.concatenate([row[:, None] * inv, col[:, None] * inv], axis=-1)
    return jnp.cos(ang), jnp.sin(ang)


def apply_rope(x, cos, sin):
    x1, x2 = jnp.split(x.astype(jnp.float32), 2, axis=-1)
    c = cos[None, :, None, :]
    s = sin[None, :, None, :]
    return jnp.concatenate([x1 * c - x2 * s, x1 * s + x2 * c], axis=-1).astype(x.dtype)


def ctx_self_attention(q, k, v, sink=None):
    B, L, Hq, d = q.shape
    Hk = k.shape[2]
    G = Hq // Hk
    qg = q.reshape(B, L, Hk, G, d)
    s = jnp.einsum('blkgd,bmkd->bkglm', qg, k).astype(jnp.float32) * ATTN_SCALE
    if sink is not None:
        sk = jnp.broadcast_to(sink.reshape(Hk, G)[None, :, :, None, None].astype(jnp.float32), s.shape[:-1] + (1,))
        s = jnp.concatenate([s, sk], axis=-1)
    p = jax.nn.softmax(s, axis=-1)[..., :L].astype(v.dtype)
    o = jnp.einsum('bkglm,bmkd->blkgd', p, v)
    return o.reshape(B, L, Hq * d)


def neighbourhood_attention(q, k, v, kc, vc, rel_bias):
    B, S, H, d = q.shape
    rows = S // GRID_W
    wh = min(NA_WIN_H, rows)
    ww = NA_WIN_W
    n_win = wh * ww
    col = jnp.arange(GRID_W)
    cs = jnp.clip(col - ww // 2, 0, GRID_W - ww)
    col_idx = cs[:, None] + jnp.arange(ww)[None, :]
    dc = col_idx - col[:, None]

    def row_block(r):
        rs = jnp.clip(r - wh // 2, 0, rows - wh)
        row_idx = rs + jnp.arange(wh)
        dr = row_idx - r
        tok = (row_idx[None, :, None] * GRID_W + col_idx[:, None, :]).reshape(GRID_W, n_win)
        kw = jnp.take(k, tok, axis=1)
        vw = jnp.take(v, tok, axis=1)
        qr = lax.dynamic_slice_in_dim(q, r * GRID_W, GRID_W, axis=1)
        bias = rel_bias[:, dr[None, :, None] + NA_WIN_H - 1, dc[:, None, :] + NA_WIN_W - 1]
        bias = bias.reshape(H, GRID_W, n_win).astype(jnp.float32)
        s_win = jnp.einsum('bqhd,bqnhd->bhqn', qr, kw).astype(jnp.float32) * ATTN_SCALE + bias[None]
        s_ctx = jnp.einsum('bqhd,bmhd->bhqm', qr, kc).astype(jnp.float32) * ATTN_SCALE
        p = jax.nn.softmax(jnp.concatenate([s_win, s_ctx], axis=-1), axis=-1).astype(v.dtype)
        return (jnp.einsum('bhqn,bqnhd->bqhd', p[..., :n_win], vw)
                + jnp.einsum('bhqm,bmhd->bqhd', p[..., n_win:], vc))

    out = lax.map(row_block, jnp.arange(rows))
    return out.transpose(1, 0, 2, 3, 4).reshape(B, S, H * d)


def sliding_window_attention(q, k, v, kc, vc, sink):
    B, S, Hq, d = q.shape
    Hk = k.shape[2]
    G = Hq // Hk
    L = kc.shape[1]
    nb = S // BLOCK
    pad = ((0, 0), (BLOCK, BLOCK), (0, 0), (0, 0))

    def bands(t):
        tp = jnp.pad(t, pad).reshape(B, nb + 2, BLOCK, Hk, d)
        return jnp.concatenate([tp[:, :-2], tp[:, 1:-1], tp[:, 2:]], axis=2)

    kb = bands(k)
    vb = bands(v)
    qb = q.reshape(B, nb, BLOCK, Hk, G, d)
    s_win = jnp.einsum('bnikgd,bnjkd->bnkgij', qb, kb).astype(jnp.float32) * ATTN_SCALE
    i = jnp.arange(BLOCK)[:, None]
    j = jnp.arange(3 * BLOCK)[None, :]
    rel = j - BLOCK - i
    kpos = jnp.arange(nb)[:, None] * BLOCK - BLOCK + jnp.arange(3 * BLOCK)[None, :]
    valid = (jnp.abs(rel) <= SW_RADIUS)[None] & ((kpos >= 0) & (kpos < S))[:, None, :]
    s_win = jnp.where(valid[None, :, None, None], s_win, NEG_INF)
    s_ctx = jnp.einsum('bnikgd,bmkd->bnkgim', qb, kc).astype(jnp.float32) * ATTN_SCALE
    s_sink = jnp.broadcast_to(sink.reshape(Hk, G)[None, None, :, :, None, None].astype(jnp.float32),
                              s_win.shape[:-1] + (1,))
    p = jax.nn.softmax(jnp.concatenate([s_win, s_ctx, s_sink], axis=-1), axis=-1).astype(v.dtype)
    nw = 3 * BLOCK
    o = (jnp.einsum('bnkgij,bnjkd->bnikgd', p[..., :nw], vb)
         + jnp.einsum('bnkgim,bmkd->bnikgd', p[..., nw:nw + L], vc))
    return o.reshape(B, S, Hq * d)


def dense_block_attention(q, k, v, kc, vc):
    B, S, Hq, d = q.shape
    Hk = k.shape[2]
    G = Hq // Hk
    nb = S // BLOCK
    kk = jnp.concatenate([k, kc], axis=1)
    vv = jnp.concatenate([v, vc], axis=1)
    qb = q.reshape(B, nb, BLOCK, Hk, G, d).transpose(1, 0, 2, 3, 4, 5)

    def blk(qi):
        s = jnp.einsum('bikgd,bjkd->bkgij', qi, kk).astype(jnp.float32) * ATTN_SCALE
        p = jax.nn.softmax(s, axis=-1).astype(vv.dtype)
        return jnp.einsum('bkgij,bjkd->bikgd', p, vv)

    o = lax.map(blk, qb)
    return o.transpose(1, 0, 2, 3, 4, 5).reshape(B, S, Hq * d)


def mixer_na_sw(h, hc, w_in, w_out, na_qg, na_kg, rel_bias, sw_qg, sw_kg, sink, cos, sin, with_ctx):
    qa_d = NA_HEADS * HEAD_DIM
    q_d = qa_d + SW_HEADS * HEAD_DIM
    kva = NA_HEADS * HEAD_DIM
    kvb = SW_KV_HEADS * HEAD_DIM

    def q_split(p):
        qa = rms_norm(heads(p[..., :qa_d], NA_HEADS), na_qg)
        qb = rms_norm(heads(p[..., qa_d:], SW_HEADS), sw_qg)
        return qa, qb

    def kv_split(p):
        ka = rms_norm(heads(p[..., :kva], NA_HEADS), na_kg)
        va = heads(p[..., kva:2 * kva], NA_HEADS)
        kb = rms_norm(heads(p[..., 2 * kva:2 * kva + kvb], SW_KV_HEADS), sw_kg)
        vb = heads(p[..., 2 * kva + kvb:], SW_KV_HEADS)
        return ka, va, kb, vb

    p = h @ w_in
    qa, qb = q_split(p[..., :q_d])
    ka, va, kb, vb = kv_split(p[..., q_d:])
    qb = apply_rope(qb, cos, sin)
    kb = apply_rope(kb, cos, sin)
    ka_c, va_c, kb_c, vb_c = kv_split(hc @ w_in[:, q_d:])
    o_a = neighbourhood_attention(qa, ka, va, ka_c, va_c, rel_bias)
    o_b = sliding_window_attention(qb, kb, vb, kb_c, vb_c, sink)
    out = jnp.concatenate([o_a, o_b], axis=-1) @ w_out
    out_c = None
    if with_ctx:
        qa_c, qb_c = q_split(hc @ w_in[:, :q_d])
        out_c = jnp.concatenate([ctx_self_attention(qa_c, ka_c, va_c),
                                 ctx_self_attention(qb_c, kb_c, vb_c, sink)], axis=-1) @ w_out
    return out, out_c


def mixer_gqa(h, hc, w_in, w_out, q_gain, k_gain, cos, sin, with_ctx):
    qd = GQA_HEADS * HEAD_DIM
    kd = GQA_KV_HEADS * HEAD_DIM
    p = h @ w_in
    q = apply_rope(rms_norm(heads(p[..., :qd], GQA_HEADS), q_gain), cos, sin)
    k = apply_rope(rms_norm(heads(p[..., qd:qd + kd], GQA_KV_HEADS), k_gain), cos, sin)
    v = heads(p[..., qd + kd:], GQA_KV_HEADS)
    pc = hc @ w_in[:, qd:]
    kc = rms_norm(heads(pc[..., :kd], GQA_KV_HEADS), k_gain)
    vc = heads(pc[..., kd:], GQA_KV_HEADS)
    out = dense_block_attention(q, k, v, kc, vc) @ w_out
    out_c = None
    if with_ctx:
        qc = rms_norm(heads(hc @ w_in[:, :qd], GQA_HEADS), q_gain)
        out_c = ctx_self_attention(qc, kc, vc) @ w_out
    return out, out_c


def setup_inputs(seed: int = 0) -> dict:
    key = jax.random.key(seed)
    ks = jax.random.split(key, 24)
    f32 = jnp.float32
    nrm = lambda k, shape, s: jax.random.normal(k, shape, f32) * s
    gain = lambda k, shape: 1.0 + 0.02 * jax.random.normal(k, shape, f32)
    return {
        'x': nrm(ks[0], (BATCH, SEQ, D_MODEL), 1.0),
        'c': nrm(ks[1], (BATCH, D_MODEL), 1.0),
        'ctx': nrm(ks[2], (BATCH, CTX_LEN, D_MODEL), 1.0),
        'c_ctx': nrm(ks[3], (D_MODEL,), 1.0),
        'adaln_w': nrm(ks[4], (DEPTH, D_MODEL, N_MOD * D_MODEL), 0.5 * D_MODEL ** -0.5),
        'adaln_b': nrm(ks[5], (DEPTH, N_MOD * D_MODEL), 0.02),
        'norm_g': gain(ks[6], (DEPTH, 3, D_MODEL)),
        'ffn_w_gate': nrm(ks[7], (DEPTH, 2, D_MODEL, D_FF), D_MODEL ** -0.5),
        'ffn_w_up': nrm(ks[8], (DEPTH, 2, D_MODEL, D_FF), D_MODEL ** -0.5),
        'ffn_w_down': nrm(ks[9], (DEPTH, 2, D_FF, D_MODEL), D_FF ** -0.5),
        'ab_w_in': nrm(ks[10], (N_EVEN, D_MODEL, AB_IN), D_MODEL ** -0.5),
        'ab_w_out': nrm(ks[11], (N_EVEN, N_HEADS * HEAD_DIM, D_MODEL), (N_HEADS * HEAD_DIM) ** -0.5),
        'na_q_gain': gain(ks[12], (N_EVEN, HEAD_DIM)),
        'na_k_gain': gain(ks[13], (N_EVEN, HEAD_DIM)),
        'na_rel_bias': nrm(ks[14], (N_EVEN, NA_HEADS, 2 * NA_WIN_H - 1, 2 * NA_WIN_W - 1), 0.1),
        'sw_q_gain': gain(ks[15], (N_EVEN, HEAD_DIM)),
        'sw_k_gain': gain(ks[16], (N_EVEN, HEAD_DIM)),
        'sw_sink': nrm(ks[17], (N_EVEN, SW_HEADS), 0.5),
        'gqa_w_in': nrm(ks[18], (N_ODD, D_MODEL, C_IN), D_MODEL ** -0.5),
        'gqa_w_out': nrm(ks[19], (N_ODD, GQA_HEADS * HEAD_DIM, D_MODEL), (GQA_HEADS * HEAD_DIM) ** -0.5),
        'gqa_q_gain': gain(ks[20], (N_ODD, HEAD_DIM)),
        'gqa_k_gain': gain(ks[21], (N_ODD, HEAD_DIM)),
    }


def reference(x, c, ctx, c_ctx, adaln_w, adaln_b, norm_g, ffn_w_gate, ffn_w_up, ffn_w_down,
              ab_w_in, ab_w_out, na_q_gain, na_k_gain, na_rel_bias, sw_q_gain, sw_k_gain, sw_sink,
              gqa_w_in, gqa_w_out, gqa_q_gain, gqa_k_gain):
    S = x.shape[1]
    cos, sin = axial_rope_tables(S)
    xc = ctx
    for i in range(DEPTH):
        with_ctx = i < DEPTH - 1
        m = [t[:, None, :] for t in jnp.split(jax.nn.silu(c) @ adaln_w[i] + adaln_b[i], N_MOD, axis=-1)]
        mc = jnp.split(jax.nn.silu(c_ctx) @ adaln_w[i] + adaln_b[i], N_MOD, axis=-1)
        wa = (ffn_w_gate[i, 0], ffn_w_up[i, 0], ffn_w_down[i, 0])
        x = x + MACARON_WEIGHT * m[2] * swiglu(modulate(x, norm_g[i, 0], m[0], m[1]), *wa)
        xc = xc + MACARON_WEIGHT * mc[2] * swiglu(modulate(xc, norm_g[i, 0], mc[0], mc[1]), *wa)
        h = modulate(x, norm_g[i, 1], m[3], m[4])
        hc = modulate(xc, norm_g[i, 1], mc[3], mc[4])
        if i % 2 == 0:
            e = i // 2
            out, out_c = mixer_na_sw(h, hc, ab_w_in[e], ab_w_out[e], na_q_gain[e], na_k_gain[e],
                                     na_rel_bias[e], sw_q_gain[e], sw_k_gain[e], sw_sink[e],
                                     cos, sin, with_ctx)
        else:
            o = i // 2
            out, out_c = mixer_gqa(h, hc, gqa_w_in[o], gqa_w_out[o], gqa_q_gain[o], gqa_k_gain[o],
                                   cos, sin, with_ctx)
        x = x + m[5] * out
        wb = (ffn_w_gate[i, 1], ffn_w_up[i, 1], ffn_w_down[i, 1])
        x = x + MACARON_WEIGHT * m[8] * swiglu(modulate(x, norm_g[i, 2], m[6], m[7]), *wb)
        if with_ctx:
            xc = xc + mc[5] * out_c
            xc = xc + MACARON_WEIGHT * mc[8] * swiglu(modulate(xc, norm_g[i, 2], mc[6], mc[7]), *wb)
    return x
```

```python
import numpy as np
from contextlib import ExitStack
import ml_dtypes
import concourse.bass as bass
import concourse.mybir as mybir
from concourse.bass_utils import run_bass_kernel_spmd

F32 = mybir.dt.float32
BF16 = mybir.dt.bfloat16
ALU = mybir.AluOpType
AF = mybir.ActivationFunctionType

NCORES = 8
D = 2048
DC = 16
DFF = 5632
NL = 1024
NCX = 32
NT = NL + NCX
SEQ = 8192
CTX = 256
GRID_W = 64
TILES = [(0, 512, 0), (512, 1024, 0), (1024, 1056, 1)]
SCALE = float(128 ** -0.5)
NEG = -30000.0
EPS = 1e-6
NWIN = 14


class Buf:
    __slots__ = ("name", "lw", "rd", "dsem", "dcnt", "persistent")

    def __init__(self, name, persistent=False):
        self.name = name
        self.persistent = persistent
        self.lw = None
        self.rd = {}
        self.dsem = None
        self.dcnt = 0


class Eng:
    def __init__(self, name, sem):
        self.name = name
        self.sem = sem
        self.cnt = 0
        self.ops = []
        self.waited = {}


class K:
    def __init__(self, nc, stack):
        self.nc = nc
        self.stack = stack
        self.E = {}
        for nm in ("pe", "act", "dve", "pool", "sp"):
            sem = stack.enter_context(nc.semaphore("s_" + nm))
            self.E[nm] = Eng(nm, sem)
        self.dbufs = []
        self.nsem = 5
        self.free_sems = []

    def _deps(self, E, reads, writes):
        deps = {}

        def add(ev):
            if ev is None:
                return
            s, v = ev
            kk = id(s)
            if kk not in deps or deps[kk][1] < v:
                deps[kk] = (s, v)
        for b in reads:
            add(b.lw)
        for b in writes:
            add(b.lw)
            for ev in b.rd.values():
                add(ev)
        waits = []
        for kk, (s, v) in deps.items():
            if E.name == "pe" and s is E.sem:
                continue
            if E.waited.get(kk, 0) >= v:
                continue
            E.waited[kk] = v
            waits.append((s, v))
        return waits

    def op(self, eng, fn, reads=(), writes=(), signal=True):
        E = self.E[eng]
        waits = self._deps(E, reads, writes)
        ev = None
        inc = None
        if signal:
            E.cnt += 1
            ev = (E.sem, E.cnt)
            inc = (E.sem, 1)
        E.ops.append((waits, fn, inc))
        if ev is not None:
            for b in reads:
                b.rd[id(ev[0])] = ev
            for b in writes:
                b.lw = ev
                b.rd = {}
        return ev

    def mm(self, out_ap, b_out, pairs, rbufs):
        n = len(pairs)
        for i, (l, r) in enumerate(pairs):
            self.op("pe", R.matmul(out_ap, lhsT=l, rhs=r, start=(i == 0), stop=(i == n - 1)),
                    reads=rbufs, writes=[b_out], signal=(i == n - 1))

    def dma(self, eng, out, in_, reads=(), writes=(), sem_of=None, **kw):
        E = self.E[eng]
        waits = self._deps(E, reads, writes)
        sb = sem_of if sem_of is not None else writes[0]
        if sb.dsem is None:
            if self.free_sems:
                sb.dsem, sb.dcnt = self.free_sems.pop()
            else:
                sb.dsem = self.stack.enter_context(self.nc.semaphore("d%d" % self.nsem))
                self.nsem += 1
            self.dbufs.append(sb)
        sb.dcnt += 16
        ev = (sb.dsem, sb.dcnt)
        E.ops.append((waits, R.dma_start(out=out, in_=in_, **kw), (sb.dsem, 16)))
        for b in reads:
            b.rd[id(ev[0])] = ev
        for b in writes:
            b.lw = ev
            b.rd = {}
        return ev

    def recycle(self):
        keep = []
        for b in self.dbufs:
            if b.persistent:
                keep.append(b)
            else:
                self.free_sems.append((b.dsem, b.dcnt))
                b.dsem = None
                b.dcnt = 0
                b.lw = None
                b.rd = {}
        self.dbufs = keep

    def barrier(self):
        evs = []
        for E in self.E.values():
            if E.cnt > 0:
                evs.append((E.sem, E.cnt))
        for b in self.dbufs:
            evs.append((b.dsem, b.dcnt))
        for E in self.E.values():
            waits = []
            for s, v in evs:
                if s is E.sem:
                    continue
                if E.waited.get(id(s), 0) >= v:
                    continue
                E.waited[id(s)] = v
                waits.append((s, v))
            E.ops.append((waits, None, None))

    def emit(self):
        nc = self.nc
        with nc.Block() as block:
            def run(E):
                def body(e):
                    for waits, fn, inc in E.ops:
                        for s, v in waits:
                            e.wait_ge(s, v)
                        if fn is None:
                            continue
                        if isinstance(fn, tuple):
                            ins = getattr(e, fn[0])(*fn[1], **fn[2])
                        else:
                            ins = fn(e)
                        if inc is not None:
                            ins.then_inc(inc[0], inc[1])
                return body
            block.sync(run(self.E["sp"]))
            block.tensor(run(self.E["pe"]))
            block.scalar(run(self.E["act"]))
            block.vector(run(self.E["dve"]))
            block.gpsimd(run(self.E["pool"]))
        for E in self.E.values():
            E.ops = []


class _Rec:
    def __getattr__(self, name):
        def f(*a, **kw):
            return (name, a, kw)
        return f


R = _Rec()


class Rot:
    def __init__(self, items):
        self.items = items
        self.i = 0

    def next(self):
        it = self.items[self.i % len(self.items)]
        self.i += 1
        return it


class P:
    def __init__(self):
        self.nc = bass.Bass("TRN2", target_bir_lowering=False)
        self.gst = ExitStack()
        self.k = K(self.nc, self.gst)
        nc = self.nc
        self.PS = [self.gst.enter_context(nc.psum_tensor("ps%d" % i, [128, 512], F32)) for i in range(8)]
        self.bPS = [Buf("ps%d" % i) for i in range(8)]
        self.outbufs = []
        self.nm = 0

    def sb(self, shape, dt, st=None, name=None):
        self.nm += 1
        st = st if st is not None else self.gst
        return st.enter_context(self.nc.sbuf_tensor(name or ("t%d" % self.nm), shape, dt))

    def din(self, name, shape, dt=F32):
        return self.nc.dram_tensor(name, list(shape), dt, kind="ExternalInput").ap()

    def dout(self, name, shape, dt=F32):
        ap = self.nc.dram_tensor(name, list(shape), dt, kind="ExternalOutput").ap()
        return ap

    def psr(self, idxs):
        return Rot([(self.PS[i], self.bPS[i]) for i in idxs])

    def allgather(self, src2d, dst2d, bdst):
        k = self.k
        if getattr(self, "ccsem", None) is None:
            self.ccsem = self.gst.enter_context(self.nc.semaphore("ccsem"))
            self.cccnt = 0
        self.cccnt += 1
        k.E["pool"].ops.append(([], R.collective_compute("AllGather", ALU.bypass, replica_groups=[list(range(NCORES))],
                                                         ins=[src2d.opt()], outs=[dst2d.opt()]), (self.ccsem, 1)))
        bdst.lw = (self.ccsem, self.cccnt)
        bdst.rd = {}

    def adaln(self, aw, ab, cc, modo):
        p = self
        k = self.k
        st = ExitStack()
        bsc = Buf("sc")
        scf = p.sb([128, 32], F32, st)
        p.load_cols(cc.rearrange("s (k p) -> (s k) p", p=128), 32, scf[:], bsc, st)
        k.op("act", R.activation(out=scf[:], in_=scf[:], func=AF.Silu), reads=[bsc], writes=[bsc])
        bT = p.sb([128, 36], F32, st)
        bbT = Buf("bT")
        p.load_cols(ab.rearrange("l (c p) -> (l c) p", p=128), 36, bT[:], bbT, st)
        mo = p.sb([128, 2, 18, 2], F32, st)
        bmo = Buf("mo")
        sc3 = scf[:].rearrange("p (s k) -> p s k", s=2)
        W = Rot([(p.sb([128, DC, 256], F32, st), Buf("aw%d" % i)) for i in range(4)])
        for l in range(2):
            ps, bps = p.PS[1 + l], p.bPS[1 + l]
            for i in range(9):
                wt, bw = W.next()
                k.dma("sp", wt[:], aw[l, :, i * 256:(i + 1) * 256].rearrange("(k p) n -> p k n", p=128), writes=[bw])
                for c3 in range(2):
                    ch = i * 2 + c3
                    k.mm(ps[:, ch * 2:(ch + 1) * 2], bps, [(wt[:, kk, c3 * 128:(c3 + 1) * 128], sc3[:, :, kk]) for kk in range(DC)],
                         [bw, bsc])
            for s_ in range(2):
                k.op("dve", R.tensor_tensor(
                    out=mo[:, l, :, s_], in0=ps[:, 0:36].rearrange("p (c s) -> p c s", s=2)[:, :, s_], in1=bT[:, l * 18:(l + 1) * 18],
                    op=ALU.add), reads=[bps, bbT], writes=[bmo])
        k.dma("sp", modo, mo[:].rearrange("p l c s -> p (l c s)"), reads=[bmo], sem_of=bmo)
        p.end_phase(st)

    def end_phase(self, st):
        self.k.barrier()
        self.k.recycle()
        self.k.emit()
        st.close()

    def finish(self):
        k = self.k
        k.barrier()
        k.emit()
        self.gst.close()
        return self.nc

    def consts(self):
        k = self.k
        self.ident = self.sb([128, 128], F32)
        self.ones = self.sb([128, 128], F32)
        self.onesb = self.sb([128, 128], BF16)
        self.bC = Buf("consts", True)
        ident, ones, onesb = self.ident, self.ones, self.onesb
        k.op("pool", R.memset(ident[:], 0.0), writes=[self.bC])
        k.op("pool", R.affine_select(out=ident[:], in_=ident[:], pattern=[[-1, 128]],
                                               compare_op=ALU.not_equal, fill=1.0, base=0,
                                               channel_multiplier=1), reads=[self.bC], writes=[self.bC])
        k.op("dve", R.memset(ones[:], 1.0), writes=[self.bC])
        k.op("dve", R.memset(onesb[:], 1.0), writes=[self.bC])
        self.scr = Rot([(self.sb([128, 512], F32), Buf("scr%d" % i)) for i in range(7)])
        self.hb = Rot([(self.sb([128, 512], BF16), Buf("hb%d" % i)) for i in range(5)])
        self.rstd = (self.sb([128, 512], F32), Buf("rstd"))

    def state(self):
        self.xT = self.sb([128, DC, NT], F32, name="xT_sb")
        self.hT = self.sb([128, DC, NT], BF16, name="hT_sb")
        self.bX = [[Buf("x%d_%d" % (dc, ti), True) for ti in range(3)] for dc in range(DC)]
        self.bH = [Buf("h%d" % ti, True) for ti in range(3)]

    def load_cols(self, rows_ap, nr, dst_ap, b_dst, st):
        k = self.k
        tmp = self.sb([128, 128], F32, st)
        bt = Buf("lc")
        k.dma("sp", tmp[:nr, :], rows_ap, writes=[bt])
        ps, bps = self.PS[0], self.bPS[0]
        ident = self.ident
        k.op("pe", R.transpose(ps[:, 0:nr], tmp[:nr, :], ident[:nr, :nr]), reads=[bt, self.bC], writes=[bps])
        k.op("dve", R.tensor_copy(dst_ap, ps[:, 0:nr]), reads=[bps], writes=[b_dst])

    def load_mod(self, mod_ap, normg_ap, layers, gm=None):
        k = self.k
        st = ExitStack()
        self.mod = self.sb([128, 2, 144, 2], F32, name="mod_sb")
        self.modh = self.sb([128, 2, 144, 2], F32, name="modh_sb")
        self.gT = self.sb([128, 96], F32, name="gT_sb")
        self.Amod = self.sb([128, 6, DC, 2], F32, name="Amod_sb")
        self.bM = Buf("mod", True)
        mod, modh, gT, Amod = self.mod, self.modh, self.gT, self.Amod
        if gm is None:
            k.dma("sp", mod[:], mod_ap, writes=[self.bM])
        else:
            g_mod, bGm = gm
            for r in range(NCORES):
                k.dma("sp", mod[:, :, r * 18:(r + 1) * 18, :],
                      g_mod[r * 128:(r + 1) * 128, :].rearrange("p (l c s) -> p l c s", l=2, c=18), reads=[bGm], writes=[self.bM])
        k.op("dve", R.tensor_scalar(out=modh[:].rearrange("p a b c -> p (a b c)"),
                                              in0=mod[:].rearrange("p a b c -> p (a b c)"),
                                              scalar1=0.5, scalar2=None, op0=ALU.mult),
             reads=[self.bM], writes=[self.bM])
        self.load_cols(normg_ap.rearrange("l n (c p) -> (l n c) p", p=128), 96, gT[:], self.bM, st)
        for l in layers:
            for n in range(3):
                for s in range(2):
                    k.op("dve", R.scalar_tensor_tensor(
                        out=Amod[:, l * 3 + n, :, s], in0=mod[:, l, (3 * n + 1) * 16:(3 * n + 2) * 16, s],
                        scalar=1.0, in1=gT[:, (l * 3 + n) * 16:(l * 3 + n + 1) * 16],
                        op0=ALU.add, op1=ALU.mult), reads=[self.bM], writes=[self.bM])
        self.end_phase(st)

    def m_shift(self, l, n, dc, s):
        return self.mod[:, l, 3 * n * 16 + dc, s:s + 1]

    def m_A(self, l, n, dc, s):
        return self.Amod[:, l * 3 + n, dc, s:s + 1]

    def m_gate(self, l, n, dc, s):
        src = self.mod if n == 1 else self.modh
        return src[:, l, (3 * n + 2) * 16 + dc, s:s + 1]

    def load_x_tokmajor(self, x_ap):
        k = self.k
        st = ExitStack()
        xs = Rot([(self.sb([128, 1024], F32, st), Buf("xs%d" % i)) for i in range(2)])
        psr = self.psr([0, 1, 2, 3])
        ident, xT = self.ident, self.xT
        cnt = 0
        for tc in range(9):
            tok0 = tc * 128
            tw = 128 if tc < 8 else NCX
            ti = min(tc // 4, 2)
            for half in range(2):
                t, bt = xs.next()
                k.dma("sp", t[:tw, :], x_ap[tok0:tok0 + tw, half * 1024:(half + 1) * 1024], writes=[bt])
                for q4 in range(2):
                    ps, bps = psr.next()
                    for j in range(4):
                        k.op("pe", R.transpose(
                            ps[:, j * 128:j * 128 + tw], t[:tw, (q4 * 4 + j) * 128:(q4 * 4 + j + 1) * 128], ident[:tw, :tw]),
                            reads=[bt, self.bC], writes=[bps], signal=(j == 3))
                    dc0 = half * 8 + q4 * 4
                    wb = [self.bX[dc0 + j][ti] for j in range(4)]
                    src = ps[:, 0:512].rearrange("p (j t) -> p j t", j=4)[:, :, 0:tw]
                    dst = xT[:, dc0:dc0 + 4, tok0:tok0 + tw]
                    if cnt % 2 == 0:
                        k.op("act", R.copy(dst, src), reads=[bps], writes=wb)
                    else:
                        k.op("dve", R.tensor_copy(dst, src), reads=[bps], writes=wb)
                    cnt += 1
        self.end_phase(st)

    def load_xT(self, xT_ap):
        k = self.k
        for dc in range(DC):
            k.dma("sp", self.xT[:, dc, :], xT_ap[:, dc, :], writes=self.bX[dc])

    def store_xT(self, xT_ap):
        k = self.k
        for dc in range(DC):
            k.dma("sp", xT_ap[:, dc, :], self.xT[:, dc, :], reads=self.bX[dc], sem_of=self.bX[dc][0])

    def store_x_tokmajor(self, out_ap):
        k = self.k
        st = ExitStack()
        osb = Rot([(self.sb([128, 1024], F32, st), Buf("os%d" % i)) for i in range(2)])
        psr = self.psr([0, 1, 2, 3])
        ident, xT = self.ident, self.xT
        cnt = 0
        for tc in range(8):
            ti = tc // 4
            for half in range(2):
                t, bt = osb.next()
                for q4 in range(2):
                    ps, bps = psr.next()
                    for j in range(4):
                        dc = half * 8 + q4 * 4 + j
                        k.op("pe", R.transpose(
                            ps[:, j * 128:(j + 1) * 128], xT[:, dc, tc * 128:(tc + 1) * 128], ident[:]),
                            reads=[self.bX[dc][ti], self.bC], writes=[bps], signal=True)
                    dst = t[:, q4 * 512:(q4 + 1) * 512]
                    if cnt % 2 == 0:
                        k.op("act", R.copy(dst, ps[:, 0:512]), reads=[bps], writes=[bt])
                    else:
                        k.op("dve", R.tensor_copy(dst, ps[:, 0:512]), reads=[bps], writes=[bt])
                    cnt += 1
                k.dma("sp", out_ap[tc * 128:(tc + 1) * 128, half * 1024:(half + 1) * 1024], t[:], reads=[bt], sem_of=bt)
        self.end_phase(st)

    def norm_mod(self, l, n, tiles):
        k = self.k
        xT, hT = self.xT, self.hT
        ones = self.ones
        psr = self.psr([0, 1])
        for ti, (c0, c1, s) in enumerate(tiles):
            w = c1 - c0
            ps, bps = psr.next()
            for dc in range(DC):
                sq, bsq = self.scr.next()
                k.op("act", R.activation(out=sq[:, :w], in_=xT[:, dc, c0:c1], func=AF.Square),
                     reads=[self.bX[dc][ti]], writes=[bsq])
                k.op("pe", R.matmul(ps[:, :w], lhsT=ones[:], rhs=sq[:, :w],
                                                                   start=(dc == 0), stop=(dc == DC - 1)),
                     reads=[bsq, self.bC], writes=[bps], signal=True)
            r, br = self.rstd
            k.op("dve", R.tensor_scalar(out=r[:, :w], in0=ps[:, :w], scalar1=1.0 / D, scalar2=EPS,
                                                              op0=ALU.mult, op1=ALU.add), reads=[bps], writes=[br])
            k.op("act", R.activation(out=r[:, :w], in_=r[:, :w], func=AF.Sqrt), reads=[br], writes=[br])
            k.op("dve", R.reciprocal(out=r[:, :w], in_=r[:, :w]), reads=[br], writes=[br])
            for dc in range(DC):
                t, bt = self.scr.next()
                k.op("dve", R.scalar_tensor_tensor(
                    out=t[:, :w], in0=xT[:, dc, c0:c1], scalar=self.m_A(l, n, dc, s), in1=r[:, :w],
                    op0=ALU.mult, op1=ALU.mult), reads=[self.bX[dc][ti], br, self.bM], writes=[bt])
                k.op("act", R.activation(out=hT[:, dc, c0:c1], in_=t[:, :w], func=AF.Identity,
                                                               bias=self.m_shift(l, n, dc, s), scale=1.0),
                     reads=[bt, self.bM], writes=[self.bH[ti]])

    def ffn(self, l, n, tiles, wg, wu, wd, ng=None):
        k = self.k
        st = ExitStack()
        xT, hT = self.xT, self.hT
        NB = 2
        Wg = [self.sb([128, DC, 256], BF16, st) for _ in range(NB)]
        Wu = [self.sb([128, DC, 256], BF16, st) for _ in range(NB)]
        Wd = [self.sb([128, 2, D], BF16, st) for _ in range(NB)]
        bWg = [Buf("wg%d" % i) for i in range(NB)]
        bWu = [Buf("wu%d" % i) for i in range(NB)]
        bWd = [Buf("wd%d" % i) for i in range(NB)]
        A = [self.sb([128, 2, NT], BF16, st) for _ in range(2)]
        bA = [[Buf("a%d_%d" % (i, ti)) for ti in range(3)] for i in range(2)]
        pg, pu, pd = self.psr([0, 1]), self.psr([2, 3]), self.psr([4, 5, 6, 7])
        NG = ng if ng is not None else DFF // 256

        def load(fg):
            sl = fg % NB
            k.dma("pool", Wg[sl][:], wg[:, fg * 256:(fg + 1) * 256].rearrange("(k p) n -> p k n", p=128), writes=[bWg[sl]])
            k.dma("pool", Wu[sl][:], wu[:, fg * 256:(fg + 1) * 256].rearrange("(k p) n -> p k n", p=128), writes=[bWu[sl]])
            k.dma("pool", Wd[sl][:], wd[fg * 256:(fg + 1) * 256, :].rearrange("(f p) n -> p f n", p=128), writes=[bWd[sl]])

        load(0)
        for fg in range(NG):
            sl = fg % NB
            if fg + 1 < NG:
                load(fg + 1)
            a = A[fg % 2]
            ba = bA[fg % 2]
            for ti, (c0, c1, s) in enumerate(tiles):
                w = c1 - c0
                for fc in range(2):
                    g_ps, bg = pg.next()
                    k.mm(g_ps[:, :w], bg, [(Wg[sl][:, kk, fc * 128:(fc + 1) * 128], hT[:, kk, c0:c1]) for kk in range(DC)],
                         [bWg[sl], self.bH[ti]])
                    u_ps, bu = pu.next()
                    k.mm(u_ps[:, :w], bu, [(Wu[sl][:, kk, fc * 128:(fc + 1) * 128], hT[:, kk, c0:c1]) for kk in range(DC)],
                         [bWu[sl], self.bH[ti]])
                    sg, bsg = self.scr.next()
                    k.op("act", R.activation(out=sg[:, :w], in_=g_ps[:, :w], func=AF.Silu),
                         reads=[bg], writes=[bsg])
                    k.op("dve", R.tensor_tensor(
                        out=a[:, fc, c0:c1], in0=sg[:, :w], in1=u_ps[:, :w], op=ALU.mult),
                        reads=[bsg, bu], writes=[ba[ti]])
                for dc in range(DC):
                    d_ps, bd = pd.next()
                    k.mm(d_ps[:, :w], bd, [(Wd[sl][:, fc, dc * 128:(dc + 1) * 128], a[:, fc, c0:c1]) for fc in range(2)],
                         [bWd[sl], ba[ti]])
                    k.op("dve", R.scalar_tensor_tensor(
                        out=xT[:, dc, c0:c1], in0=d_ps[:, :w], scalar=self.m_gate(l, n, dc, s), in1=xT[:, dc, c0:c1],
                        op0=ALU.mult, op1=ALU.add), reads=[bd, self.bM, self.bX[dc][ti]], writes=[self.bX[dc][ti]])
        self.end_phase(st)

    def alloc_pw(self, st):
        if getattr(st, "_pw", None) is None:
            st._pw = ([self.sb([128, DC, 256], BF16, st) for _ in range(2)], [Buf("pw%d" % i) for i in range(2)])
        return st._pw

    def proj_fm(self, w_ap, col0, nchunks, tiles, handler, st, src=None, bsrc=None):
        k = self.k
        src = src if src is not None else self.hT
        bsrc = bsrc if bsrc is not None else self.bH
        NB = 2
        W, bW = self.alloc_pw(st)
        pm = self.psr([0, 1, 2, 3])
        ng = (nchunks + 1) // 2

        def load(g):
            cw = min(2, nchunks - g * 2) * 128
            k.dma("pool", W[g % NB][:, :, 0:cw], w_ap[:, col0 + g * 256:col0 + g * 256 + cw].rearrange("(k p) n -> p k n", p=128),
                  writes=[bW[g % NB]])
        load(0)
        for g in range(ng):
            if g + 1 < ng:
                load(g + 1)
            sl = g % NB
            for ci in range(min(2, nchunks - g * 2)):
                ch = g * 2 + ci
                for ti, (c0, c1, s) in enumerate(tiles):
                    w = c1 - c0
                    ps, bps = pm.next()
                    k.mm(ps[:, :w], bps, [(W[sl][:, kk, ci * 128:(ci + 1) * 128], src[:, kk, c0:c1]) for kk in range(DC)],
                         [bW[sl], bsrc[ti]])
                    handler(ch, ti, (c0, c1, s), ps, bps)

    def proj_tm(self, w_ap, col0, ncols, v_out, st, nchunks_tok=9):
        k = self.k
        hT = self.hT
        NB = 2
        W, bW = self.alloc_pw(st)
        if getattr(st, "_vo", None) is None:
            st._vo = Rot([(self.sb([128, 256], BF16, st), Buf("vo%d" % i)) for i in range(3)])
        vo = st._vo
        pm = self.psr([4, 5, 6, 7])
        ng = ncols // 256

        def load(g):
            k.dma("pool", W[g % NB][:], w_ap[:, col0 + g * 256:col0 + (g + 1) * 256].rearrange("(k p) n -> p k n", p=128),
                  writes=[bW[g % NB]])
        load(0)
        for g in range(ng):
            if g + 1 < ng:
                load(g + 1)
            sl = g % NB
            for tc in range(nchunks_tok):
                tok0 = tc * 128
                tw = 128 if tc < 8 else NCX
                ti = min(tc // 4, 2)
                ps, bps = pm.next()
                k.mm(ps[:tw, 0:256], bps, [(hT[:, kk, tok0:tok0 + tw], W[sl][:, kk, :]) for kk in range(DC)],
                     [bW[sl], self.bH[ti]])
                t, bt = vo.next()
                k.op("act", R.copy(t[:tw, :], ps[:tw, 0:256]), reads=[bps], writes=[bt])
                k.dma("sp", v_out[tok0:tok0 + tw, g * 256:(g + 1) * 256], t[:tw, :], reads=[bt], sem_of=bt)

    def qk_post(self, ps, bps, tile, gain_ap, rope, out_t, b_out, cos=None, sin=None, ssr=None):
        k = self.k
        c0, c1, s = tile
        w = c1 - c0
        ones = self.ones
        qf, bqf = self.scr.next()
        sq, bsq = self.scr.next()
        k.op("act", R.copy(qf[:, :w], ps[:, :w]), reads=[bps], writes=[bqf])
        k.op("act", R.activation(out=sq[:, :w], in_=ps[:, :w], func=AF.Square), reads=[bps], writes=[bsq])
        ss, bss = ssr.next()
        k.op("pe", R.matmul(ss[:, :w], lhsT=ones[:], rhs=sq[:, :w], start=True, stop=True),
             reads=[bsq, self.bC], writes=[bss])
        r, br = self.scr.next()
        k.op("dve", R.tensor_scalar(out=r[:, :w], in0=ss[:, :w], scalar1=1.0 / 128, scalar2=EPS,
                                              op0=ALU.mult, op1=ALU.add), reads=[bss], writes=[br])
        k.op("act", R.activation(out=r[:, :w], in_=r[:, :w], func=AF.Sqrt), reads=[br], writes=[br])
        k.op("dve", R.reciprocal(out=r[:, :w], in_=r[:, :w]), reads=[br], writes=[br])
        if rope and s == 0:
            qn, bqn = self.scr.next()
            k.op("dve", R.scalar_tensor_tensor(out=qn[:, :w], in0=qf[:, :w], scalar=gain_ap, in1=r[:, :w],
                                                         op0=ALU.mult, op1=ALU.mult), reads=[bqf, br, self.bG], writes=[bqn])
            t1, bt1 = self.scr.next()
            t2, bt2 = self.scr.next()
            k.op("pool", R.tensor_tensor(out=t1[:, :w], in0=qn[:, :w], in1=cos[:, c0:c1], op=ALU.mult),
                 reads=[bqn, self.bR], writes=[bt1])
            k.op("pool", R.tensor_tensor(out=t2[0:64, :w], in0=qn[64:128, :w], in1=sin[64:128, c0:c1], op=ALU.mult),
                 reads=[bqn, self.bR], writes=[bt2])
            k.op("pool", R.tensor_tensor(out=t2[64:128, :w], in0=qn[0:64, :w], in1=sin[0:64, c0:c1], op=ALU.mult),
                 reads=[bqn, self.bR], writes=[bt2])
            k.op("dve", R.tensor_tensor(out=out_t[0:64, c0:c1], in0=t1[0:64, :w], in1=t2[0:64, :w], op=ALU.subtract),
                 reads=[bt1, bt2], writes=[b_out])
            k.op("dve", R.tensor_tensor(out=out_t[64:128, c0:c1], in0=t1[64:128, :w], in1=t2[64:128, :w], op=ALU.add),
                 reads=[bt1, bt2], writes=[b_out])
        else:
            k.op("dve", R.scalar_tensor_tensor(out=out_t[:, c0:c1], in0=qf[:, :w], scalar=gain_ap, in1=r[:, :w],
                                                         op0=ALU.mult, op1=ALU.mult), reads=[bqf, br, self.bG], writes=[b_out])

    def load_gains(self, gains, st):
        k = self.k
        self.gn = self.sb([128, len(gains)], F32, st)
        self.bG = Buf("gains")
        for i, g in enumerate(gains):
            k.dma("sp", self.gn[:, i:i + 1], g.rearrange("(p o) -> p o", o=1), writes=[self.bG])

    def load_rope(self, cos_ap, sin_ap, st):
        k = self.k
        self.cos = self.sb([128, NL], F32, st)
        self.sin = self.sb([128, NL], F32, st)
        self.bR = Buf("rope")
        k.dma("sp", self.cos[:], cos_ap, writes=[self.bR])
        k.dma("sp", self.sin[:], sin_ap, writes=[self.bR])

    def proj_l0(self, w_in, gains, cos_ap, sin_ap, outs):
        k = self.k
        st = ExitStack()
        self.load_gains(gains, st)
        self.load_rope(cos_ap, sin_ap, st)
        qo = Rot([(self.sb([128, NT], BF16, st), Buf("qo%d" % i)) for i in range(2)])
        ssr = self.psr([4, 5])
        groups = [("qa", 0, 8, 0, False), ("qb", 8, 8, 2, True), ("ka", 16, 8, 1, False), ("kb", 32, 2, 3, True)]
        for nm, ch0, nch, gi, rope in groups:
            cur = {}

            def handler(ch, ti, tile, ps, bps, nm=nm, gi=gi, rope=rope, cur=cur):
                if ti == 0:
                    cur["t"] = qo.next()
                t, bt = cur["t"]
                self.qk_post(ps, bps, tile, self.gn[:, gi:gi + 1], rope, t, bt, self.cos, self.sin, ssr)
                if ti == len(TILES) - 1:
                    k.dma("sp", outs[nm][ch, :, :], t[:], reads=[bt], sem_of=bt)
            self.proj_fm(w_in, ch0 * 128, nch, TILES, handler, st)
        self.proj_tm(w_in, 24 * 128, 1024, outs["va"], st)
        self.proj_tm(w_in, 34 * 128, 256, outs["vb"], st)
        self.end_phase(st)

    def proj_l1(self, w_in, gains, cos_ap, sin_ap, outs):
        k = self.k
        st = ExitStack()
        self.load_gains(gains, st)
        self.load_rope(cos_ap, sin_ap, st)
        qo = Rot([(self.sb([128, NT], BF16, st), Buf("qo%d" % i)) for i in range(2)])
        ssr = self.psr([4, 5])
        cur = {}

        def hq(ch, ti, tile, ps, bps):
            if ti == 0:
                cur["t"] = qo.next()
            t, bt = cur["t"]
            self.qk_post(ps, bps, tile, self.gn[:, 0:1], True, t, bt, self.cos, self.sin, ssr)
            if ti == 1:
                k.dma("sp", outs["q"][ch, :, :], t[:, 0:NL], reads=[bt], sem_of=bt)

        def hk(ch, ti, tile, ps, bps):
            if ti == 0:
                cur["t"] = qo.next()
            t, bt = cur["t"]
            self.qk_post(ps, bps, tile, self.gn[:, 1:2], True, t, bt, self.cos, self.sin, ssr)
            if ti == 2:
                k.dma("sp", outs["k"][ch, :, :], t[:], reads=[bt], sem_of=bt)
        self.proj_fm(w_in, 0, 16, TILES[:2], hq, st)
        self.proj_fm(w_in, 16 * 128, 4, TILES, hk, st)
        self.proj_tm(w_in, 20 * 128, 512, outs["v"], st)
        self.end_phase(st)

    def out_proj(self, w_out, l, tiles):
        k = self.k
        st = ExitStack()
        xT = self.xT

        def handler(ch, ti, tile, ps, bps):
            c0, c1, s = tile
            w = c1 - c0
            k.op("dve", R.scalar_tensor_tensor(
                out=xT[:, ch, c0:c1], in0=ps[:, :w], scalar=self.m_gate(l, 1, ch, s), in1=xT[:, ch, c0:c1],
                op0=ALU.mult, op1=ALU.add), reads=[bps, self.bM, self.bX[ch][ti]], writes=[self.bX[ch][ti]])
        self.proj_fm(w_out, 0, 16, tiles, handler, st)
        self.end_phase(st)

    def attn_l0(self, qa, qb, ka, kb, va, vb, nab, swm_ap, sink_ap, fused=None):
        k = self.k
        st = ExitStack()
        oT = self.hT
        onesb = self.onesb
        kT = [(self.sb([128, NWIN * 128], BF16, st), Buf("kT%d" % i)) for i in range(2)]
        vv = [(self.sb([128, NWIN, 128], BF16, st), Buf("vv%d" % i)) for i in range(2)]
        qq = [(self.sb([128, 4, NT], BF16, st), Buf("qq%d" % i)) for i in range(2)]
        bm = Rot([(self.sb([128, 6, 128], F32, st), Buf("bm%d" % i)) for i in range(2)])
        swm = self.sb([128, 8, 2, 128], F32, st)
        bswm = Buf("swm")
        esink = self.sb([128, 8], F32, st)
        bes = Buf("esink")
        k.dma("sp", swm[:], swm_ap, writes=[bswm])
        k.dma("sp", esink[:], sink_ap.rearrange("(o n) -> o n", o=1).partition_broadcast(128), writes=[bes])
        k.op("act", R.activation(out=esink[:], in_=esink[:], func=AF.Exp), reads=[bes], writes=[bes])
        pS = self.psr([0, 1, 2, 3])
        pO = self.psr([4, 5])
        pD = self.psr([6, 7])
        bO = self.bH
        if fused is not None:
            shK, gK, shV, gV, selp_ap, seln_ap, bG = fused
            gK3 = gK.rearrange("(r x) t -> x r t", r=NCORES)
            gV3 = gV.rearrange("(r t) c -> t r c", r=NCORES)
            sel = self.sb([128, 2, NCORES], F32, st)
            bsel = Buf("sel")
            k.dma("sp", sel[:, 0, :], selp_ap, writes=[bsel])
            k.dma("sp", sel[:, 1, :], seln_ap, writes=[bsel])
            candK = Rot([(self.sb([128, NCORES, 256], BF16, st), Buf("ck%d" % i)) for i in range(2)])
            candV = Rot([(self.sb([128, NCORES, 2, 128], BF16, st), Buf("cv%d" % i)) for i in range(2)])

            def select(dst, cand, bcand, side, bdst):
                k.op("dve", R.tensor_scalar(out=dst, in0=cand(0), scalar1=sel[:, side, 0:1], scalar2=None, op0=ALU.mult),
                     reads=[bcand, bsel], writes=[bdst])
                for r in range(1, NCORES):
                    k.op("dve", R.scalar_tensor_tensor(out=dst, in0=cand(r), scalar=sel[:, side, r:r + 1], in1=dst,
                                                        op0=ALU.mult, op1=ALU.add), reads=[bcand, bsel], writes=[bdst])

            def load_k(hk, kt, bkt):
                k.dma("sp", kt[:, 256:1280], shK[hk, :, 0:NL], writes=[bkt])
                k.dma("sp", kt[:, 1536:1792].rearrange("p (r t) -> p r t", r=NCORES),
                      gK3[hk * 128:(hk + 1) * 128, :, NL:NT], reads=[bG], writes=[bkt])
                for side, (c0, c1, d0) in enumerate(((768, 1024, 0), (0, 256, 1280))):
                    ct, bct = candK.next()
                    k.dma("sp", ct[:], gK3[hk * 128:(hk + 1) * 128, :, c0:c1], reads=[bG], writes=[bct])
                    select(kt[:, d0:d0 + 256], (lambda r, ct=ct: ct[:, r, :]), bct, side, bkt)

            def load_v(hv, vt, bvt):
                cols = slice(hv * 128, (hv + 1) * 128)
                k.dma("sp", vt[:, 2:10, :], shV[0:NL, cols].rearrange("(j p) d -> p j d", p=128), writes=[bvt])
                for r in range(NCORES):
                    k.dma("sp", vt[(r % 4) * 32:(r % 4 + 1) * 32, 12 + r // 4, :], gV[r * NT + NL:(r + 1) * NT, cols],
                          reads=[bG], writes=[bvt])
                for side, (t0, w0) in enumerate(((768, 0), (0, 10))):
                    ct, bct = candV.next()
                    for j in range(2):
                        k.dma("sp", ct[:, :, j, :], gV3[t0 + j * 128:t0 + (j + 1) * 128, :, cols], reads=[bG], writes=[bct])
                    select(vt[:, w0:w0 + 2, :], (lambda r, ct=ct: ct[:, r, :, :]), bct, side, bvt)
        else:
            def load_k(hk, kt, bkt):
                src = ka[hk, :, :] if hk < 8 else kb[hk - 8, :, :]
                k.dma("sp", kt[:], src, writes=[bkt])

            def load_v(hv, vt, bvt):
                src = va[hv, :, :, :] if hv < 8 else vb[hv - 8, :, :, :]
                k.dma("sp", vt[:], src, writes=[bvt])

        def finalize(o_ps, bo, d_ps, bd, w3, out_ap, ti, sink_heads=None):
            rd, brd = self.scr.next()
            if sink_heads is None:
                k.op("dve", R.reciprocal(out=rd[:, :w3], in_=d_ps[:, :w3]), reads=[bd], writes=[brd])
            else:
                wq = w3 // 4
                for j, h in enumerate(sink_heads):
                    k.op("dve", R.tensor_scalar(
                        out=rd[:, j * wq:(j + 1) * wq], in0=d_ps[:, j * wq:(j + 1) * wq], scalar1=esink[:, h:h + 1],
                        scalar2=None, op0=ALU.add), reads=[bd, bes], writes=[brd])
                k.op("dve", R.reciprocal(out=rd[:, :w3], in_=rd[:, :w3]), reads=[brd], writes=[brd])
            if sink_heads is None:
                k.op("dve", R.tensor_tensor(out=out_ap, in0=o_ps[:, :w3], in1=rd[:, :w3], op=ALU.mult),
                     reads=[bo, brd], writes=[bO[ti]])
            else:
                wq = w3 // 4
                k.op("dve", R.tensor_tensor(
                    out=out_ap, in0=o_ps[:, :w3].rearrange("p (h q) -> p h q", h=4),
                    in1=rd[:, :w3].rearrange("p (h q) -> p h q", h=4), op=ALU.mult),
                    reads=[bo, brd], writes=[bO[ti]])

        for h in range(8):
            kt, bkt = kT[h % 2]
            vt, bvt = vv[h % 2]
            qt, bqt = qq[h % 2]
            load_k(h, kt, bkt)
            load_v(h, vt, bvt)
            k.dma("sp", qt[:, 0, :], qa[h, :, :], writes=[bqt])
            for jt in range(9):
                ti = min(jt // 4, 2)
                q0 = jt * 128
                qw = 128 if jt < 8 else NCX
                groups = []
                if jt < 8:
                    w0 = min(jt, 6)
                    bt_, bbt = bm.next()
                    k.dma("sp", bt_[:], nab[jt, h, :, :, :], writes=[bbt])
                    groups.append(([w0, w0 + 1, w0 + 2], bt_[:, 0:3, :], bbt))
                    groups.append(([w0 + 3, w0 + 4, w0 + 5], bt_[:, 3:6, :], bbt))
                groups.append(([12, 13], None, None))
                o_ps, bo = pO.next()
                d_ps, bd = pD.next()
                nmm = sum(len(g[0]) for g in groups)
                im = 0
                for wl, bias, bbias in groups:
                    n = len(wl)
                    s_ps, bs = pS.next()
                    for i, wi in enumerate(wl):
                        k.op("pe", R.matmul(
                            s_ps[:, i * qw:(i + 1) * qw], lhsT=kt[:, wi * 128:(wi + 1) * 128], rhs=qt[:, 0, q0:q0 + qw],
                            start=True, stop=True), reads=[bkt, bqt], writes=[bs], signal=(i == n - 1))
                    p_t, bp = self.hb.next()
                    if bias is not None:
                        t, bt = self.scr.next()
                        k.op("dve", R.scalar_tensor_tensor(
                            out=t[:, :n * qw].rearrange("p (s q) -> p s q", s=n), in0=s_ps[:, :n * qw].rearrange("p (s q) -> p s q", s=n),
                            scalar=SCALE, in1=bias, op0=ALU.mult, op1=ALU.add), reads=[bs, bbias], writes=[bt])
                        k.op("act", R.activation(out=p_t[:, :n * qw], in_=t[:, :n * qw], func=AF.Exp),
                             reads=[bt], writes=[bp])
                    else:
                        k.op("act", R.activation(out=p_t[:, :n * qw], in_=s_ps[:, :n * qw],
                                                                                   func=AF.Exp, scale=SCALE),
                             reads=[bs], writes=[bp])
                    for i, wi in enumerate(wl):
                        first = (im == 0)
                        last = (im == nmm - 1)
                        k.op("pe", R.matmul(
                            o_ps[:, :qw], lhsT=vt[:, wi, :], rhs=p_t[:, i * qw:(i + 1) * qw], start=first, stop=last),
                            reads=[bvt, bp], writes=[bo], signal=last)
                        k.op("pe", R.matmul(
                            d_ps[:, :qw], lhsT=onesb[:], rhs=p_t[:, i * qw:(i + 1) * qw], start=first, stop=last),
                            reads=[bp, self.bC], writes=[bd], signal=(last or i == n - 1))
                        im += 1
                finalize(o_ps, bo, d_ps, bd, qw, oT[:, h, q0:q0 + qw], ti)

        for kv in range(2):
            kt, bkt = kT[kv % 2]
            vt, bvt = vv[kv % 2]
            qt, bqt = qq[kv % 2]
            load_k(8 + kv, kt, bkt)
            load_v(8 + kv, vt, bvt)
            for j in range(4):
                k.dma("sp", qt[:, j, :], qb[kv * 4 + j, :, :], writes=[bqt])
            for jt in range(9):
                ti = min(jt // 4, 2)
                q0 = jt * 128
                qw = 128 if jt < 8 else NCX
                steps = []
                if jt < 8:
                    steps.append((jt + 1, 0))
                    steps.append((jt + 2, None))
                    steps.append((jt + 3, 1))
                steps.append((12, None))
                steps.append((13, None))
                o_ps, bo = pO.next()
                d_ps, bd = pD.next()
                w4 = 4 * qw
                for si, (wi, mside) in enumerate(steps):
                    s_ps, bs = pS.next()
                    k.op("pe", R.matmul(
                        s_ps[:, :w4].rearrange("p (h q) -> p h q", h=4), lhsT=kt[:, wi * 128:(wi + 1) * 128],
                        rhs=qt[:, :, q0:q0 + qw], start=True, stop=True), reads=[bkt, bqt], writes=[bs])
                    p_t, bp = self.hb.next()
                    if mside is not None:
                        t, bt = self.scr.next()
                        for j in range(4):
                            k.op("dve", R.scalar_tensor_tensor(
                                out=t[:, j * qw:(j + 1) * qw], in0=s_ps[:, j * qw:(j + 1) * qw], scalar=SCALE,
                                in1=swm[:, jt, mside, :], op0=ALU.mult, op1=ALU.add), reads=[bs, bswm], writes=[bt])
                        k.op("act", R.activation(out=p_t[:, :w4], in_=t[:, :w4], func=AF.Exp),
                             reads=[bt], writes=[bp])
                    else:
                        k.op("act", R.activation(out=p_t[:, :w4], in_=s_ps[:, :w4], func=AF.Exp,
                                                                              scale=SCALE), reads=[bs], writes=[bp])
                    first = (si == 0)
                    last = (si == len(steps) - 1)
                    k.op("pe", R.matmul(
                        o_ps[:, :w4], lhsT=vt[:, wi, :], rhs=p_t[:, :w4], start=first, stop=last),
                        reads=[bvt, bp], writes=[bo], signal=last)
                    k.op("pe", R.matmul(
                        d_ps[:, :w4], lhsT=onesb[:], rhs=p_t[:, :w4], start=first, stop=last),
                        reads=[bp, self.bC], writes=[bd], signal=True)
                finalize(o_ps, bo, d_ps, bd, w4, oT[:, 8 + kv * 4:8 + kv * 4 + 4, q0:q0 + qw], ti,
                         sink_heads=[kv * 4 + j for j in range(4)])
        self.end_phase(st)

    def attn_l1(self, q_ap, k_ap, v_ap, fused=None):
        k = self.k
        st = ExitStack()
        oT = self.hT
        onesb = self.onesb
        NK = 66
        kT = [(self.sb([128, NK * 128], BF16, st), Buf("kT%d" % i)) for i in range(2)]
        vv = [(self.sb([128, NK, 128], BF16, st), Buf("vv%d" % i)) for i in range(2)]
        qq = Rot([(self.sb([128, NL], BF16, st), Buf("qq%d" % i)) for i in range(2)])
        pS = self.psr([0, 1, 2])
        pO = self.psr([3, 4])
        pD = self.psr([5, 6])
        for kv in range(4):
            kt, bkt = kT[kv % 2]
            vt, bvt = vv[kv % 2]
            if fused is None:
                for part in range(2):
                    k.dma("sp", kt[:, part * 33 * 128:(part + 1) * 33 * 128], k_ap[kv, :, part * 33 * 128:(part + 1) * 33 * 128],
                          writes=[bkt])
                for part in range(2):
                    k.dma("sp", vt[:, part * 33:(part + 1) * 33, :], v_ap[kv, :, part * 33:(part + 1) * 33, :], writes=[bvt])
            else:
                gK, gV, bG = fused
                gK3 = gK.rearrange("(r x) t -> x r t", r=NCORES)
                for half in range(2):
                    k.dma("sp", kt[:, half * 4096:(half + 1) * 4096].rearrange("p (r t) -> p r t", r=4),
                          gK3[kv * 128:(kv + 1) * 128, half * 4:(half + 1) * 4, 0:NL], reads=[bG], writes=[bkt])
                k.dma("sp", kt[:, 8192:8448].rearrange("p (r t) -> p r t", r=NCORES),
                      gK3[kv * 128:(kv + 1) * 128, :, NL:NT], reads=[bG], writes=[bkt])
                cols = slice(kv * 128, (kv + 1) * 128)
                for r in range(NCORES):
                    k.dma("sp", vt[:, r * 8:(r + 1) * 8, :], gV[r * NT:r * NT + NL, cols].rearrange("(j p) d -> p j d", p=128),
                          reads=[bG], writes=[bvt])
                    k.dma("sp", vt[(r % 4) * 32:(r % 4 + 1) * 32, 64 + r // 4, :], gV[r * NT + NL:(r + 1) * NT, cols],
                          reads=[bG], writes=[bvt])
            for hq in range(4):
                qt, bqt = qq.next()
                k.dma("sp", qt[:], q_ap[kv * 4 + hq, :, :], writes=[bqt])
                for qt2 in range(2):
                    q0 = qt2 * 512
                    rhs_q = qt[:, q0:q0 + 512]
                    o_ps, bo = pO.next()
                    d_ps, bd = pD.next()
                    pend = None

                    def s_step(kc):
                        s_ps, bs = pS.next()
                        k.op("pe", R.matmul(s_ps[:, :], lhsT=kt[:, kc * 128:(kc + 1) * 128], rhs=rhs_q,
                                                      start=True, stop=True), reads=[bkt, bqt], writes=[bs])
                        p_t, bp = self.hb.next()
                        k.op("act", R.activation(out=p_t[:, :], in_=s_ps[:, :], func=AF.Exp, scale=SCALE),
                             reads=[bs], writes=[bp])
                        return (kc, p_t, bp)

                    def pv_step(item):
                        kc, p_t, bp = item
                        first = (kc == 0)
                        last = (kc == NK - 1)
                        k.op("pe", R.matmul(o_ps[:, :], lhsT=vt[:, kc, :], rhs=p_t[:, :], start=first, stop=last),
                             reads=[bvt, bp], writes=[bo], signal=last)
                        k.op("pe", R.matmul(d_ps[:, :], lhsT=onesb[:], rhs=p_t[:, :], start=first, stop=last),
                             reads=[bp, self.bC], writes=[bd], signal=True)

                    pend = s_step(0)
                    for kc in range(1, NK):
                        nxt = s_step(kc)
                        pv_step(pend)
                        pend = nxt
                    pv_step(pend)
                    rd, brd = self.scr.next()
                    k.op("dve", R.reciprocal(out=rd[:, :], in_=d_ps[:, :]), reads=[bd], writes=[brd])
                    k.op("dve", R.tensor_tensor(
                        out=oT[:, kv * 4 + hq, q0:q0 + 512], in0=o_ps[:, :], in1=rd[:, :], op=ALU.mult),
                        reads=[bo, brd], writes=[self.bH[qt2]])
        self.end_phase(st)


def build_A():
    p = P()
    k = p.k
    aw = p.din("aw", [2, D, 2304])
    ab = p.din("ab", [2, 2304])
    cc = p.din("cc", [2, D])
    modo = p.dout("modo", [128, 2, 18, 2])
    p.consts()
    st = ExitStack()
    bsc = Buf("sc")
    scf = p.sb([128, 32], F32, st)
    p.load_cols(cc.rearrange("s (k p) -> (s k) p", p=128), 32, scf[:], bsc, st)
    k.op("act", R.activation(out=scf[:], in_=scf[:], func=AF.Silu), reads=[bsc], writes=[bsc])
    bT = p.sb([128, 36], F32, st)
    bbT = Buf("bT")
    p.load_cols(ab.rearrange("l (c p) -> (l c) p", p=128), 36, bT[:], bbT, st)
    mo = p.sb([128, 2, 18, 2], F32, st)
    bmo = Buf("mo")
    sc3 = scf[:].rearrange("p (s k) -> p s k", s=2)
    W = Rot([(p.sb([128, DC, 384], F32, st), Buf("aw%d" % i)) for i in range(4)])
    for l in range(2):
        ps, bps = p.PS[1 + l], p.bPS[1 + l]
        for i in range(6):
            wt, bw = W.next()
            k.dma("sp", wt[:], aw[l, :, i * 384:(i + 1) * 384].rearrange("(k p) n -> p k n", p=128), writes=[bw])
            for c3 in range(3):
                ch = i * 3 + c3
                k.mm(ps[:, ch * 2:(ch + 1) * 2], bps, [(wt[:, kk, c3 * 128:(c3 + 1) * 128], sc3[:, :, kk]) for kk in range(DC)],
                     [bw, bsc])
        for s in range(2):
            k.op("dve", R.tensor_tensor(
                out=mo[:, l, :, s], in0=ps[:, 0:36].rearrange("p (c s) -> p c s", s=2)[:, :, s], in1=bT[:, l * 18:(l + 1) * 18],
                op=ALU.add), reads=[bps, bbT], writes=[bmo])
    k.dma("sp", modo, mo[:], reads=[bmo], sem_of=bmo)
    p.end_phase(st)
    return p.finish()


def _common_inputs(p):
    mod = p.din("mod", [128, 2, 144, 2])
    normg = p.din("normg", [2, 3, D])
    return mod, normg


def build_B():
    p = P()
    x_in = p.din("x_in", [NT, D])
    mod, normg = _common_inputs(p)
    wg = p.din("wg", [D, DFF]); wu = p.din("wu", [D, DFF]); wd = p.din("wd", [DFF, D])
    w_in = p.din("w_in", [D, 4608])
    gains = [p.din(n, [128]) for n in ("g_naq", "g_nak", "g_swq", "g_swk")]
    cos = p.din("cos", [128, NL]); sin = p.din("sin", [128, NL])
    xT_o = p.dout("xT_o", [128, DC, NT])
    outs = {"qa": p.dout("qa", [8, 128, NT], BF16), "qb": p.dout("qb", [8, 128, NT], BF16),
            "ka": p.dout("ka", [8, 128, NT], BF16), "kb": p.dout("kb", [2, 128, NT], BF16),
            "va": p.dout("va", [NT, 1024], BF16), "vb": p.dout("vb", [NT, 256], BF16)}
    p.consts()
    p.state()
    p.load_mod(mod, normg, [0])
    p.load_x_tokmajor(x_in)
    p.norm_mod(0, 0, TILES)
    p.ffn(0, 0, TILES, wg, wu, wd)
    p.norm_mod(0, 1, TILES)
    p.store_xT(xT_o)
    p.proj_l0(w_in, gains, cos, sin, outs)
    return p.finish()


def build_C():
    p = P()
    xT_i = p.din("xT_i", [128, DC, NT])
    mod, normg = _common_inputs(p)
    qa = p.din("qa", [8, 128, NT], BF16); qb = p.din("qb", [8, 128, NT], BF16)
    ka = p.din("kaw", [8, 128, NWIN * 128], BF16); kb = p.din("kbw", [2, 128, NWIN * 128], BF16)
    va = p.din("vaw", [8, 128, NWIN, 128], BF16); vb = p.din("vbw", [2, 128, NWIN, 128], BF16)
    nab = p.din("nab", [8, 8, 128, 6, 128]); swm = p.din("swm", [128, 8, 2, 128]); sink = p.din("sink", [8])
    w_out = p.din("w_out", [D, D])
    wg0 = p.din("wg0", [D, DFF]); wu0 = p.din("wu0", [D, DFF]); wd0 = p.din("wd0", [DFF, D])
    wg1 = p.din("wg1", [D, DFF]); wu1 = p.din("wu1", [D, DFF]); wd1 = p.din("wd1", [DFF, D])
    w_in = p.din("w_in", [D, 3072])
    gains = [p.din(n, [128]) for n in ("g_q", "g_k")]
    cos = p.din("cos", [128, NL]); sin = p.din("sin", [128, NL])
    xT_o = p.dout("xT_o", [128, DC, NT])
    outs = {"q": p.dout("q", [16, 128, NL], BF16), "k": p.dout("k", [4, 128, NT], BF16), "v": p.dout("v", [NT, 512], BF16)}
    p.consts()
    p.state()
    p.load_mod(mod, normg, [0, 1])
    p.load_xT(xT_i)
    p.attn_l0(qa, qb, ka, kb, va, vb, nab, swm, sink)
    p.out_proj(w_out, 0, TILES)
    p.norm_mod(0, 2, TILES)
    p.ffn(0, 2, TILES, wg0, wu0, wd0)
    p.norm_mod(1, 0, TILES)
    p.ffn(1, 0, TILES, wg1, wu1, wd1)
    p.norm_mod(1, 1, TILES)
    p.store_xT(xT_o)
    p.proj_l1(w_in, gains, cos, sin, outs)
    return p.finish()


def build_D():
    p = P()
    xT_i = p.din("xT_i", [128, DC, NT])
    mod, normg = _common_inputs(p)
    q = p.din("q", [16, 128, NL], BF16)
    kk = p.din("kall", [4, 128, 66 * 128], BF16)
    vv = p.din("vall", [4, 128, 66, 128], BF16)
    w_out = p.din("w_out", [D, D])
    wg = p.din("wg", [D, DFF]); wu = p.din("wu", [D, DFF]); wd = p.din("wd", [DFF, D])
    out = p.dout("out", [NL, D])
    p.consts()
    p.state()
    p.load_mod(mod, normg, [1])
    p.load_xT(xT_i)
    p.attn_l1(q, kk, vv)
    p.out_proj(w_out, 1, TILES[:2])
    p.norm_mod(1, 2, TILES[:2])
    p.ffn(1, 2, TILES[:2], wg, wu, wd)
    p.store_x_tokmajor(out)
    return p.finish()


def build_F():
    p = P()
    nc = p.nc
    aw = p.din("aw", [2, D, 2304]); ab = p.din("ab", [2, 2304]); cc = p.din("cc", [2, D])
    x_in = p.din("x_in", [NT, D])
    normg = p.din("normg", [2, 3, D])
    W = {}
    for l in range(2):
        for w_ in range(2):
            W[(l, w_)] = (p.din("wg%d%d" % (l, w_), [D, DFF]), p.din("wu%d%d" % (l, w_), [D, DFF]), p.din("wd%d%d" % (l, w_), [DFF, D]))
    ab_in = p.din("ab_in", [D, 4608]); ab_out = p.din("ab_out", [D, D])
    g_in = p.din("g_in", [D, 3072]); g_out = p.din("g_out", [D, D])
    gains0 = [p.din(n, [128]) for n in ("g_naq", "g_nak", "g_swq", "g_swk")]
    gains1 = [p.din(n, [128]) for n in ("g_q", "g_k")]
    cos = p.din("cos", [128, NL]); sin = p.din("sin", [128, NL])
    nab = p.din("nab", [8, 8, 128, 6, 128]); swm = p.din("swm", [128, 8, 2, 128]); sink = p.din("sink", [8])
    selp = p.din("selp", [128, NCORES]); seln = p.din("seln", [128, NCORES])
    out = p.dout("out", [NL, D])
    dt_ = lambda name, shape, dt: nc.dram_tensor(name, list(shape), dt).ap()
    sh_mod = dt_("sh_mod", [128, 72], F32); g_mod = dt_("g_mod", [NCORES * 128, 72], F32)
    qa_d = dt_("qa_d", [8, 128, NT], BF16); qb_d = dt_("qb_d", [8, 128, NT], BF16)
    shK0 = dt_("shK0", [10, 128, NT], BF16); gK0 = dt_("gK0", [NCORES * 1280, NT], BF16)
    shV0 = dt_("shV0", [NT, 1280], BF16); gV0 = dt_("gV0", [NCORES * NT, 1280], BF16)
    q1_d = dt_("q1_d", [16, 128, NL], BF16)
    shK1 = dt_("shK1", [4, 128, NT], BF16); gK1 = dt_("gK1", [NCORES * 512, NT], BF16)
    shV1 = dt_("shV1", [NT, 512], BF16); gV1 = dt_("gV1", [NCORES * NT, 512], BF16)
    bGm, bG0, bG1 = Buf("gmod"), Buf("g0"), Buf("g1")

    p.consts()
    p.state()
    p.adaln(aw, ab, cc, sh_mod)
    p.allgather(sh_mod, g_mod, bGm)
    p.load_mod(None, normg, [0, 1], gm=(g_mod, bGm))
    p.load_x_tokmajor(x_in)
    p.norm_mod(0, 0, TILES)
    p.ffn(0, 0, TILES, *W[(0, 0)])
    p.norm_mod(0, 1, TILES)
    p.proj_l0(ab_in, gains0, cos, sin, {"qa": qa_d, "qb": qb_d, "ka": shK0[0:8], "kb": shK0[8:10],
                                        "va": shV0[:, 0:1024], "vb": shV0[:, 1024:1280]})
    p.allgather(shK0.rearrange("h d t -> (h d) t"), gK0, bG0)
    p.allgather(shV0, gV0, bG0)
    p.attn_l0(qa_d, qb_d, None, None, None, None, nab, swm, sink, fused=(shK0, gK0, shV0, gV0, selp, seln, bG0))
    p.out_proj(ab_out, 0, TILES)
    p.norm_mod(0, 2, TILES)
    p.ffn(0, 2, TILES, *W[(0, 1)])
    p.norm_mod(1, 0, TILES)
    p.ffn(1, 0, TILES, *W[(1, 0)])
    p.norm_mod(1, 1, TILES)
    p.proj_l1(g_in, gains1, cos, sin, {"q": q1_d, "k": shK1, "v": shV1})
    p.allgather(shK1.rearrange("h d t -> (h d) t"), gK1, bG1)
    p.allgather(shV1, gV1, bG1)
    p.attn_l1(q1_d, None, None, fused=(gK1, gV1, bG1))
    p.out_proj(g_out, 1, TILES[:2])
    p.norm_mod(1, 2, TILES[:2])
    p.ffn(1, 2, TILES[:2], *W[(1, 1)])
    p.store_x_tokmajor(out)
    return p.finish()


_PROGS = {}


def _prog(name):
    if name not in _PROGS:
        _PROGS[name] = {"A": build_A, "B": build_B, "C": build_C, "D": build_D, "F": build_F}[name]()
    return _PROGS[name]


def _run(name, in_maps):
    res = run_bass_kernel_spmd(_prog(name), in_maps, core_ids=list(range(NCORES)))
    return res.results


def _rope_tables():
    t = np.arange(SEQ)
    row = (t // GRID_W).astype(np.float32)
    col = (t % GRID_W).astype(np.float32)
    inv = (np.float32(10000.0) ** (-np.arange(0, 64, 2, dtype=np.float32) / np.float32(64))).astype(np.float32)
    ang = np.concatenate([row[:, None] * inv, col[:, None] * inv], axis=-1).astype(np.float32)
    cos = np.cos(ang).astype(np.float32)
    sin = np.sin(ang).astype(np.float32)
    cosT = np.concatenate([cos, cos], axis=1).T
    sinT = np.concatenate([sin, sin], axis=1).T
    return np.ascontiguousarray(cosT), np.ascontiguousarray(sinT)


def _na_bias(rel_bias, c):
    out = np.full((8, 8, 128, 6, 128), NEG, np.float32)
    kl = np.arange(128)
    ql = np.arange(128)
    for jt in range(8):
        j = 8 * c + jt
        qr = 2 * j + ql // 64
        qc = ql % 64
        rs = np.clip(qr - 4, 0, 120)
        cs = np.clip(qc - 8, 0, 48)
        w0 = min(jt, 6)
        for s in range(6):
            gch = 8 * c - 2 + w0 + s
            if gch < 0 or gch > 63:
                continue
            kr = 2 * gch + kl // 64
            kc = kl % 64
            valid = ((kr[:, None] >= rs[None, :]) & (kr[:, None] < rs[None, :] + 8) &
                     (kc[:, None] >= cs[None, :]) & (kc[:, None] < cs[None, :] + 16))
            dr = np.clip(kr[:, None] - qr[None, :] + 7, 0, 14)
            dc = np.clip(kc[:, None] - qc[None, :] + 15, 0, 30)
            g = rel_bias[:, dr, dc]
            out[jt, :, :, s, :] = np.where(valid[None], g, np.float32(NEG))
    return out


def _sw_mask(c):
    out = np.zeros((128, 8, 2, 128), np.float32)
    kl = np.arange(128)[:, None]
    ql = np.arange(128)[None, :]
    for jt in range(8):
        j = 8 * c + jt
        m0 = np.where(kl >= ql, 0.0, NEG).astype(np.float32)
        m1 = np.where(kl <= ql, 0.0, NEG).astype(np.float32)
        if j - 1 < 0:
            m0[:] = NEG
        if j + 1 > 63:
            m1[:] = NEG
        out[:, jt, 0, :] = m0
        out[:, jt, 1, :] = m1
    return out


def _windows(k_cores, heads):
    lat = np.concatenate([kc[:, :, :NL] for kc in k_cores], axis=2)
    ctx = np.concatenate([kc[:, :, NL:] for kc in k_cores], axis=2)
    res = []
    for c in range(NCORES):
        w = np.zeros((heads, 128, NWIN * 128), lat.dtype)
        lo = (8 * c - 2) * 128
        hi = lo + 12 * 128
        slo, shi = max(lo, 0), min(hi, SEQ)
        w[:, :, slo - lo:shi - lo] = lat[:, :, slo:shi]
        w[:, :, 12 * 128:] = ctx
        res.append(w)
    return res


def _vwindows(v_cores, heads):
    lat = np.concatenate([v[:NL] for v in v_cores], axis=0)
    ctx = np.concatenate([v[NL:] for v in v_cores], axis=0)
    res = []
    for c in range(NCORES):
        w = np.zeros((NWIN * 128, heads * 128), lat.dtype)
        lo = (8 * c - 2) * 128
        hi = lo + 12 * 128
        slo, shi = max(lo, 0), min(hi, SEQ)
        w[slo - lo:shi - lo] = lat[slo:shi]
        w[12 * 128:] = ctx
        w = w.reshape(NWIN, 128, heads, 128).transpose(2, 1, 0, 3)
        res.append(np.ascontiguousarray(w))
    return res


def kernel(x, c, ctx, c_ctx, adaln_w, adaln_b, norm_g, ffn_w_gate, ffn_w_up, ffn_w_down,
           ab_w_in, ab_w_out, na_q_gain, na_k_gain, na_rel_bias, sw_q_gain, sw_k_gain, sw_sink,
           gqa_w_in, gqa_w_out, gqa_q_gain, gqa_k_gain):
    f = lambda a: np.ascontiguousarray(np.asarray(a, dtype=np.float32))
    x = f(x); ctx = f(ctx); c = f(c); c_ctx = f(c_ctx)
    adaln_w = np.asarray(adaln_w, dtype=np.float32); adaln_b = f(adaln_b); norm_g = f(norm_g)
    cosT, sinT = _rope_tables()
    cc = np.ascontiguousarray(np.stack([c[0], c_ctx], axis=0))
    shared = {"cc": cc, "normg": norm_g, "ab_in": f(ab_w_in[0]), "ab_out": f(ab_w_out[0]), "g_in": f(gqa_w_in[0]),
              "g_out": f(gqa_w_out[0]), "g_naq": f(na_q_gain[0]), "g_nak": f(na_k_gain[0]), "g_swq": f(sw_q_gain[0]),
              "g_swk": f(sw_k_gain[0]), "g_q": f(gqa_q_gain[0]), "g_k": f(gqa_k_gain[0]), "sink": f(sw_sink[0])}
    for l in range(2):
        for w_ in range(2):
            shared["wg%d%d" % (l, w_)] = f(np.asarray(ffn_w_gate)[l, w_])
            shared["wu%d%d" % (l, w_)] = f(np.asarray(ffn_w_up)[l, w_])
            shared["wd%d%d" % (l, w_)] = f(np.asarray(ffn_w_down)[l, w_])
    rel = f(na_rel_bias[0])
    ins = []
    for i in range(NCORES):
        m = dict(shared)
        m["aw"] = np.ascontiguousarray(adaln_w[:, :, i * 2304:(i + 1) * 2304])
        m["ab"] = np.ascontiguousarray(adaln_b[:, i * 2304:(i + 1) * 2304])
        m["x_in"] = np.ascontiguousarray(np.concatenate([x[0, i * NL:(i + 1) * NL], ctx[0, i * NCX:(i + 1) * NCX]], axis=0))
        m["cos"] = np.ascontiguousarray(cosT[:, i * NL:(i + 1) * NL])
        m["sin"] = np.ascontiguousarray(sinT[:, i * NL:(i + 1) * NL])
        m["nab"] = _na_bias(rel, i)
        m["swm"] = _sw_mask(i)
        sp = np.zeros((128, NCORES), np.float32)
        sn = np.zeros((128, NCORES), np.float32)
        if i > 0:
            sp[:, i - 1] = 1.0
        if i < NCORES - 1:
            sn[:, i + 1] = 1.0
        m["selp"] = sp
        m["seln"] = sn
        ins.append(m)
    rF = _run("F", ins)
    out = np.concatenate([r["out"] for r in rF], axis=0)[None]
    return np.ascontiguousarray(out.astype(np.float32))


def kernel_unfused(x, c, ctx, c_ctx, adaln_w, adaln_b, norm_g, ffn_w_gate, ffn_w_up, ffn_w_down,
           ab_w_in, ab_w_out, na_q_gain, na_k_gain, na_rel_bias, sw_q_gain, sw_k_gain, sw_sink,
           gqa_w_in, gqa_w_out, gqa_q_gain, gqa_k_gain):
    f = lambda a: np.ascontiguousarray(np.asarray(a, dtype=np.float32))
    x = f(x); ctx = f(ctx); c = f(c); c_ctx = f(c_ctx)
    adaln_w = np.asarray(adaln_w, dtype=np.float32); adaln_b = f(adaln_b); norm_g = f(norm_g)
    ffn_w_gate = np.asarray(ffn_w_gate, dtype=np.float32); ffn_w_up = np.asarray(ffn_w_up, dtype=np.float32)
    ffn_w_down = np.asarray(ffn_w_down, dtype=np.float32)
    cosT, sinT = _rope_tables()

    cc = np.ascontiguousarray(np.stack([c[0], c_ctx], axis=0))
    ins = []
    for i in range(NCORES):
        ins.append({"aw": np.ascontiguousarray(adaln_w[:, :, i * 2304:(i + 1) * 2304]),
                    "ab": np.ascontiguousarray(adaln_b[:, i * 2304:(i + 1) * 2304]), "cc": cc})
    rA = _run("A", ins)
    mod = np.ascontiguousarray(np.concatenate([r["modo"] for r in rA], axis=2))

    wg00, wu00, wd00 = f(ffn_w_gate[0, 0]), f(ffn_w_up[0, 0]), f(ffn_w_down[0, 0])
    abin = f(ab_w_in[0])
    ins = []
    for i in range(NCORES):
        x_in = np.ascontiguousarray(np.concatenate([x[0, i * NL:(i + 1) * NL], ctx[0, i * NCX:(i + 1) * NCX]], axis=0))
        ins.append({"x_in": x_in, "mod": mod, "normg": norm_g, "wg": wg00, "wu": wu00, "wd": wd00, "w_in": abin,
                    "g_naq": f(na_q_gain[0]), "g_nak": f(na_k_gain[0]), "g_swq": f(sw_q_gain[0]), "g_swk": f(sw_k_gain[0]),
                    "cos": np.ascontiguousarray(cosT[:, i * NL:(i + 1) * NL]), "sin": np.ascontiguousarray(sinT[:, i * NL:(i + 1) * NL])})
    rB = _run("B", ins)
    del wg00, wu00, wd00

    kaw = _windows([r["ka"] for r in rB], 8)
    kbw = _windows([r["kb"] for r in rB], 2)
    vaw = _vwindows([r["va"] for r in rB], 8)
    vbw = _vwindows([r["vb"] for r in rB], 2)
    rel = f(na_rel_bias[0])
    wg01, wu01, wd01 = f(ffn_w_gate[0, 1]), f(ffn_w_up[0, 1]), f(ffn_w_down[0, 1])
    wg10, wu10, wd10 = f(ffn_w_gate[1, 0]), f(ffn_w_up[1, 0]), f(ffn_w_down[1, 0])
    about = f(ab_w_out[0]); gin = f(gqa_w_in[0])
    ins = []
    for i in range(NCORES):
        ins.append({"xT_i": rB[i]["xT_o"], "mod": mod, "normg": norm_g, "qa": rB[i]["qa"], "qb": rB[i]["qb"],
                    "kaw": kaw[i], "kbw": kbw[i], "vaw": vaw[i], "vbw": vbw[i],
                    "nab": _na_bias(rel, i), "swm": _sw_mask(i), "sink": f(sw_sink[0]),
                    "w_out": about, "wg0": wg01, "wu0": wu01, "wd0": wd01, "wg1": wg10, "wu1": wu10, "wd1": wd10,
                    "w_in": gin, "g_q": f(gqa_q_gain[0]), "g_k": f(gqa_k_gain[0]),
                    "cos": np.ascontiguousarray(cosT[:, i * NL:(i + 1) * NL]), "sin": np.ascontiguousarray(sinT[:, i * NL:(i + 1) * NL])})
    rC = _run("C", ins)
    del rB, kaw, kbw, vaw, vbw, wg01, wu01, wd01, wg10, wu10, wd10

    k_lat = np.concatenate([r["k"][:, :, :NL] for r in rC], axis=2)
    k_ctx = np.concatenate([r["k"][:, :, NL:] for r in rC], axis=2)
    kall = np.ascontiguousarray(np.concatenate([k_lat, k_ctx], axis=2))
    v_all = np.concatenate([r["v"][:NL] for r in rC] + [r["v"][NL:] for r in rC], axis=0)
    vall = np.ascontiguousarray(v_all.reshape(66, 128, 4, 128).transpose(2, 1, 0, 3))
    wg11, wu11, wd11 = f(ffn_w_gate[1, 1]), f(ffn_w_up[1, 1]), f(ffn_w_down[1, 1])
    gout = f(gqa_w_out[0])
    ins = []
    for i in range(NCORES):
        ins.append({"xT_i": rC[i]["xT_o"], "mod": mod, "normg": norm_g, "q": rC[i]["q"], "kall": kall, "vall": vall,
                    "w_out": gout, "wg": wg11, "wu": wu11, "wd": wd11})
    rD = _run("D", ins)
    out = np.concatenate([r["out"] for r in rD], axis=0)[None]
    return np.ascontiguousarray(out.astype(np.float32))
```

```python
import numpy as np
from contextlib import ExitStack
import ml_dtypes
import concourse.bass as bass
import concourse.mybir as mybir
from concourse.bass_utils import run_bass_kernel_spmd

F32 = mybir.dt.float32
BF16 = mybir.dt.bfloat16
ALU = mybir.AluOpType
AF = mybir.ActivationFunctionType

NCORES = 8
D = 2048
DC = 16
DFF = 5632
NL = 1024
NCX = 32
NT = NL + NCX
SEQ = 8192
CTX = 256
GRID_W = 64
TILES = [(0, 512, 0), (512, 1024, 0), (1024, 1056, 1)]
SCALE = float(128 ** -0.5)
NEG = -30000.0
EPS = 1e-6
NWIN = 14


class Buf:
    __slots__ = ("name", "lw", "rd", "dsem", "dcnt", "persistent")

    def __init__(self, name, persistent=False):
        self.name = name
        self.persistent = persistent
        self.lw = None
        self.rd = {}
        self.dsem = None
        self.dcnt = 0


class Eng:
    def __init__(self, name, sem):
        self.name = name
        self.sem = sem
        self.cnt = 0
        self.ops = []
        self.waited = {}


class K:
    def __init__(self, nc, stack):
        self.nc = nc
        self.stack = stack
        self.E = {}
        for nm in ("pe", "act", "dve", "pool", "sp"):
            sem = stack.enter_context(nc.semaphore("s_" + nm))
            self.E[nm] = Eng(nm, sem)
        self.dbufs = []
        self.nsem = 5
        self.free_sems = []

    def _deps(self, E, reads, writes, skip_waw=None):
        deps = {}

        def add(ev):
            if ev is None:
                return
            s, v = ev
            kk = id(s)
            if kk not in deps or deps[kk][1] < v:
                deps[kk] = (s, v)
        for b in reads:
            add(b.lw)
        for b in writes:
            if not (skip_waw is not None and b.lw is not None and b.lw[0] is skip_waw):
                add(b.lw)
            for ev in b.rd.values():
                add(ev)
        waits = []
        for kk, (s, v) in deps.items():
            if E.name == "pe" and s is E.sem:
                continue
            if E.waited.get(kk, 0) >= v:
                continue
            E.waited[kk] = v
            waits.append((s, v))
        return waits

    def op(self, eng, fn, reads=(), writes=(), signal=True):
        E = self.E[eng]
        waits = self._deps(E, reads, writes)
        ev = None
        inc = None
        if signal:
            E.cnt += 1
            ev = (E.sem, E.cnt)
            inc = (E.sem, 1)
        E.ops.append((waits, fn, inc))
        if ev is not None:
            for b in reads:
                b.rd[id(ev[0])] = ev
            for b in writes:
                b.lw = ev
                b.rd = {}
        return ev

    def mm(self, out_ap, b_out, pairs, rbufs):
        n = len(pairs)
        for i, (l, r) in enumerate(pairs):
            self.op("pe", R.matmul(out_ap, lhsT=l, rhs=r, start=(i == 0), stop=(i == n - 1)),
                    reads=rbufs, writes=[b_out], signal=(i == n - 1))

    def dma(self, eng, out, in_, reads=(), writes=(), sem_of=None, nowaw=False, **kw):
        E = self.E[eng]
        sb = sem_of if sem_of is not None else writes[0]
        waits = self._deps(E, reads, writes, skip_waw=(sb.dsem if nowaw else None))
        if sb.dsem is None:
            if self.free_sems:
                sb.dsem, sb.dcnt = self.free_sems.pop()
            else:
                sb.dsem = self.stack.enter_context(self.nc.semaphore("d%d" % self.nsem))
                self.nsem += 1
            self.dbufs.append(sb)
        sb.dcnt += 16
        ev = (sb.dsem, sb.dcnt)
        E.ops.append((waits, R.dma_start(out=out, in_=in_, **kw), (sb.dsem, 16)))
        for b in reads:
            b.rd[id(ev[0])] = ev
        for b in writes:
            b.lw = ev
            b.rd = {}
        return ev

    def recycle(self):
        keep = []
        for b in self.dbufs:
            if b.persistent:
                keep.append(b)
            else:
                self.free_sems.append((b.dsem, b.dcnt))
                b.dsem = None
                b.dcnt = 0
                b.lw = None
                b.rd = {}
        self.dbufs = keep

    def barrier(self):
        evs = []
        for E in self.E.values():
            if E.cnt > 0:
                evs.append((E.sem, E.cnt))
        for b in self.dbufs:
            evs.append((b.dsem, b.dcnt))
        for E in self.E.values():
            waits = []
            for s, v in evs:
                if s is E.sem:
                    continue
                if E.waited.get(id(s), 0) >= v:
                    continue
                E.waited[id(s)] = v
                waits.append((s, v))
            E.ops.append((waits, None, None))

    def emit(self):
        nc = self.nc
        with nc.Block() as block:
            def run(E):
                def body(e):
                    for waits, fn, inc in E.ops:
                        for s, v in waits:
                            e.wait_ge(s, v)
                        if fn is None:
                            continue
                        if isinstance(fn, tuple):
                            ins = getattr(e, fn[0])(*fn[1], **fn[2])
                        else:
                            ins = fn(e)
                        if inc is not None:
                            ins.then_inc(inc[0], inc[1])
                return body
            block.sync(run(self.E["sp"]))
            block.tensor(run(self.E["pe"]))
            block.scalar(run(self.E["act"]))
            block.vector(run(self.E["dve"]))
            block.gpsimd(run(self.E["pool"]))
        for E in self.E.values():
            E.ops = []


class _Rec:
    def __getattr__(self, name):
        def f(*a, **kw):
            return (name, a, kw)
        return f


R = _Rec()


class Rot:
    def __init__(self, items):
        self.items = items
        self.i = 0

    def next(self):
        it = self.items[self.i % len(self.items)]
        self.i += 1
        return it


class P:
    def __init__(self):
        self.nc = bass.Bass("TRN2", target_bir_lowering=False)
        self.gst = ExitStack()
        self.k = K(self.nc, self.gst)
        nc = self.nc
        self.PS = [self.gst.enter_context(nc.psum_tensor("ps%d" % i, [128, 512], F32)) for i in range(8)]
        self.bPS = [Buf("ps%d" % i) for i in range(8)]
        self.outbufs = []
        self.nm = 0

    def sb(self, shape, dt, st=None, name=None):
        self.nm += 1
        st = st if st is not None else self.gst
        return st.enter_context(self.nc.sbuf_tensor(name or ("t%d" % self.nm), shape, dt))

    def din(self, name, shape, dt=F32):
        return self.nc.dram_tensor(name, list(shape), dt, kind="ExternalInput").ap()

    def dout(self, name, shape, dt=F32):
        ap = self.nc.dram_tensor(name, list(shape), dt, kind="ExternalOutput").ap()
        return ap

    def psr(self, idxs):
        return Rot([(self.PS[i], self.bPS[i]) for i in idxs])

    def allgather(self, src2d, dst2d, bdst):
        k = self.k
        if getattr(self, "ccsem", None) is None:
            self.ccsem = self.gst.enter_context(self.nc.semaphore("ccsem"))
            self.cccnt = 0
        self.cccnt += 1
        k.E["pool"].ops.append(([], R.collective_compute("AllGather", ALU.bypass, replica_groups=[list(range(NCORES))],
                                                         ins=[src2d.opt()], outs=[dst2d.opt()]), (self.ccsem, 1)))
        bdst.lw = (self.ccsem, self.cccnt)
        bdst.rd = {}

    def adaln(self, aw, ab, cc, modo):
        p = self
        k = self.k
        st = ExitStack()
        bsc = Buf("sc")
        scf = p.sb([128, 32], F32, st)
        p.load_cols(cc.rearrange("s (k p) -> (s k) p", p=128), 32, scf[:], bsc, st)
        k.op("act", R.activation(out=scf[:], in_=scf[:], func=AF.Silu), reads=[bsc], writes=[bsc])
        bT = p.sb([128, 36], F32, st)
        bbT = Buf("bT")
        p.load_cols(ab.rearrange("l (c p) -> (l c) p", p=128), 36, bT[:], bbT, st)
        mo = p.sb([128, 2, 18, 2], F32, st)
        bmo = Buf("mo")
        sc3 = scf[:].rearrange("p (s k) -> p s k", s=2)
        W = Rot([(p.sb([128, DC, 256], F32, st), Buf("aw%d" % i)) for i in range(4)])
        for l in range(2):
            ps, bps = p.PS[1 + l], p.bPS[1 + l]
            for i in range(9):
                wt, bw = W.next()
                k.dma("sp", wt[:], aw[l, :, i * 256:(i + 1) * 256].rearrange("(k p) n -> p k n", p=128), writes=[bw])
                for c3 in range(2):
                    ch = i * 2 + c3
                    k.mm(ps[:, ch * 2:(ch + 1) * 2], bps, [(wt[:, kk, c3 * 128:(c3 + 1) * 128], sc3[:, :, kk]) for kk in range(DC)],
                         [bw, bsc])
            for s_ in range(2):
                k.op("dve", R.tensor_tensor(
                    out=mo[:, l, :, s_], in0=ps[:, 0:36].rearrange("p (c s) -> p c s", s=2)[:, :, s_], in1=bT[:, l * 18:(l + 1) * 18],
                    op=ALU.add), reads=[bps, bbT], writes=[bmo])
        k.dma("sp", modo, mo[:].rearrange("p l c s -> p (l c s)"), reads=[bmo], sem_of=bmo)
        p.end_phase(st)

    def end_phase(self, st):
        self.k.barrier()
        self.k.recycle()
        self.k.emit()
        st.close()

    def finish(self):
        k = self.k
        k.barrier()
        k.emit()
        self.gst.close()
        return self.nc

    def consts(self):
        k = self.k
        self.ident = self.sb([128, 128], F32)
        self.ones = self.sb([128, 128], F32)
        self.onesb = self.sb([128, 128], BF16)
        self.bC = Buf("consts", True)
        ident, ones, onesb = self.ident, self.ones, self.onesb
        k.op("pool", R.memset(ident[:], 0.0), writes=[self.bC])
        k.op("pool", R.affine_select(out=ident[:], in_=ident[:], pattern=[[-1, 128]],
                                               compare_op=ALU.not_equal, fill=1.0, base=0,
                                               channel_multiplier=1), reads=[self.bC], writes=[self.bC])
        k.op("dve", R.memset(ones[:], 1.0), writes=[self.bC])
        k.op("dve", R.memset(onesb[:], 1.0), writes=[self.bC])
        self.scr = Rot([(self.sb([128, 512], F32), Buf("scr%d" % i)) for i in range(7)])
        self.hb = Rot([(self.sb([128, 512], BF16), Buf("hb%d" % i)) for i in range(5)])
        self.rstd = (self.sb([128, 512], F32), Buf("rstd"))

    def state(self):
        self.xT = self.sb([128, DC, NT], F32, name="xT_sb")
        self.hT = self.sb([128, DC, NT], BF16, name="hT_sb")
        self.bX = [[Buf("x%d_%d" % (dc, ti), True) for ti in range(3)] for dc in range(DC)]
        self.bH = [Buf("h%d" % ti, True) for ti in range(3)]

    def load_cols(self, rows_ap, nr, dst_ap, b_dst, st):
        k = self.k
        tmp = self.sb([128, 128], F32, st)
        bt = Buf("lc")
        k.dma("sp", tmp[:nr, :], rows_ap, writes=[bt])
        ps, bps = self.PS[0], self.bPS[0]
        ident = self.ident
        k.op("pe", R.transpose(ps[:, 0:nr], tmp[:nr, :], ident[:nr, :nr]), reads=[bt, self.bC], writes=[bps])
        k.op("dve", R.tensor_copy(dst_ap, ps[:, 0:nr]), reads=[bps], writes=[b_dst])

    def load_mod(self, mod_ap, normg_ap, layers, gm=None):
        k = self.k
        st = ExitStack()
        self.mod = self.sb([128, 2, 144, 2], F32, name="mod_sb")
        self.modh = self.sb([128, 2, 144, 2], F32, name="modh_sb")
        self.gT = self.sb([128, 96], F32, name="gT_sb")
        self.Amod = self.sb([128, 6, DC, 2], F32, name="Amod_sb")
        self.bM = Buf("mod", True)
        mod, modh, gT, Amod = self.mod, self.modh, self.gT, self.Amod
        if gm is None:
            k.dma("sp", mod[:], mod_ap, writes=[self.bM])
        else:
            g_mod, bGm = gm
            for r in range(NCORES):
                k.dma("sp", mod[:, :, r * 18:(r + 1) * 18, :],
                      g_mod[r * 128:(r + 1) * 128, :].rearrange("p (l c s) -> p l c s", l=2, c=18), reads=[bGm], writes=[self.bM])
        k.op("dve", R.tensor_scalar(out=modh[:].rearrange("p a b c -> p (a b c)"),
                                              in0=mod[:].rearrange("p a b c -> p (a b c)"),
                                              scalar1=0.5, scalar2=None, op0=ALU.mult),
             reads=[self.bM], writes=[self.bM])
        self.load_cols(normg_ap.rearrange("l n (c p) -> (l n c) p", p=128), 96, gT[:], self.bM, st)
        for l in layers:
            for n in range(3):
                for s in range(2):
                    k.op("dve", R.scalar_tensor_tensor(
                        out=Amod[:, l * 3 + n, :, s], in0=mod[:, l, (3 * n + 1) * 16:(3 * n + 2) * 16, s],
                        scalar=1.0, in1=gT[:, (l * 3 + n) * 16:(l * 3 + n + 1) * 16],
                        op0=ALU.add, op1=ALU.mult), reads=[self.bM], writes=[self.bM])
        self.end_phase(st)

    def m_shift(self, l, n, dc, s):
        return self.mod[:, l, 3 * n * 16 + dc, s:s + 1]

    def m_A(self, l, n, dc, s):
        return self.Amod[:, l * 3 + n, dc, s:s + 1]

    def m_gate(self, l, n, dc, s):
        src = self.mod if n == 1 else self.modh
        return src[:, l, (3 * n + 2) * 16 + dc, s:s + 1]

    def load_x_tokmajor(self, x_ap):
        k = self.k
        st = ExitStack()
        xs = Rot([(self.sb([128, 1024], F32, st), Buf("xs%d" % i)) for i in range(2)])
        psr = self.psr([0, 1, 2, 3])
        ident, xT = self.ident, self.xT
        cnt = 0
        for tc in range(9):
            tok0 = tc * 128
            tw = 128 if tc < 8 else NCX
            ti = min(tc // 4, 2)
            for half in range(2):
                t, bt = xs.next()
                k.dma("sp", t[:tw, :], x_ap[tok0:tok0 + tw, half * 1024:(half + 1) * 1024], writes=[bt])
                for q4 in range(2):
                    ps, bps = psr.next()
                    for j in range(4):
                        k.op("pe", R.transpose(
                            ps[:, j * 128:j * 128 + tw], t[:tw, (q4 * 4 + j) * 128:(q4 * 4 + j + 1) * 128], ident[:tw, :tw]),
                            reads=[bt, self.bC], writes=[bps], signal=(j == 3))
                    dc0 = half * 8 + q4 * 4
                    wb = [self.bX[dc0 + j][ti] for j in range(4)]
                    src = ps[:, 0:512].rearrange("p (j t) -> p j t", j=4)[:, :, 0:tw]
                    dst = xT[:, dc0:dc0 + 4, tok0:tok0 + tw]
                    if cnt % 2 == 0:
                        k.op("act", R.copy(dst, src), reads=[bps], writes=wb)
                    else:
                        k.op("dve", R.tensor_copy(dst, src), reads=[bps], writes=wb)
                    cnt += 1
        self.end_phase(st)

    def load_xT(self, xT_ap):
        k = self.k
        for dc in range(DC):
            k.dma("sp", self.xT[:, dc, :], xT_ap[:, dc, :], writes=self.bX[dc])

    def store_xT(self, xT_ap):
        k = self.k
        for dc in range(DC):
            k.dma("sp", xT_ap[:, dc, :], self.xT[:, dc, :], reads=self.bX[dc], sem_of=self.bX[dc][0])

    def store_x_tokmajor(self, out_ap):
        k = self.k
        st = ExitStack()
        osb = Rot([(self.sb([128, 1024], F32, st), Buf("os%d" % i)) for i in range(2)])
        psr = self.psr([0, 1, 2, 3])
        ident, xT = self.ident, self.xT
        cnt = 0
        for tc in range(8):
            ti = tc // 4
            for half in range(2):
                t, bt = osb.next()
                for q4 in range(2):
                    ps, bps = psr.next()
                    for j in range(4):
                        dc = half * 8 + q4 * 4 + j
                        k.op("pe", R.transpose(
                            ps[:, j * 128:(j + 1) * 128], xT[:, dc, tc * 128:(tc + 1) * 128], ident[:]),
                            reads=[self.bX[dc][ti], self.bC], writes=[bps], signal=True)
                    dst = t[:, q4 * 512:(q4 + 1) * 512]
                    if cnt % 2 == 0:
                        k.op("act", R.copy(dst, ps[:, 0:512]), reads=[bps], writes=[bt])
                    else:
                        k.op("dve", R.tensor_copy(dst, ps[:, 0:512]), reads=[bps], writes=[bt])
                    cnt += 1
                k.dma("sp", out_ap[tc * 128:(tc + 1) * 128, half * 1024:(half + 1) * 1024], t[:], reads=[bt], sem_of=bt)
        self.end_phase(st)

    def norm_mod(self, l, n, tiles):
        k = self.k
        xT, hT = self.xT, self.hT
        ones = self.ones
        psr = self.psr([0, 1])
        for ti, (c0, c1, s) in enumerate(tiles):
            w = c1 - c0
            ps, bps = psr.next()
            for dc in range(DC):
                sq, bsq = self.scr.next()
                k.op("act", R.activation(out=sq[:, :w], in_=xT[:, dc, c0:c1], func=AF.Square),
                     reads=[self.bX[dc][ti]], writes=[bsq])
                k.op("pe", R.matmul(ps[:, :w], lhsT=ones[:], rhs=sq[:, :w],
                                                                   start=(dc == 0), stop=(dc == DC - 1)),
                     reads=[bsq, self.bC], writes=[bps], signal=True)
            r, br = self.rstd
            k.op("dve", R.tensor_scalar(out=r[:, :w], in0=ps[:, :w], scalar1=1.0 / D, scalar2=EPS,
                                                              op0=ALU.mult, op1=ALU.add), reads=[bps], writes=[br])
            k.op("act", R.activation(out=r[:, :w], in_=r[:, :w], func=AF.Sqrt), reads=[br], writes=[br])
            k.op("dve", R.reciprocal(out=r[:, :w], in_=r[:, :w]), reads=[br], writes=[br])
            for dc in range(DC):
                t, bt = self.scr.next()
                k.op("dve", R.scalar_tensor_tensor(
                    out=t[:, :w], in0=xT[:, dc, c0:c1], scalar=self.m_A(l, n, dc, s), in1=r[:, :w],
                    op0=ALU.mult, op1=ALU.mult), reads=[self.bX[dc][ti], br, self.bM], writes=[bt])
                k.op("act", R.activation(out=hT[:, dc, c0:c1], in_=t[:, :w], func=AF.Identity,
                                                               bias=self.m_shift(l, n, dc, s), scale=1.0),
                     reads=[bt, self.bM], writes=[self.bH[ti]])

    def ffn(self, l, n, tiles, wg, wu, wd, ng=None):
        k = self.k
        st = ExitStack()
        xT, hT = self.xT, self.hT
        NB = 2
        Wg = [self.sb([128, DC, 256], BF16, st) for _ in range(NB)]
        Wu = [self.sb([128, DC, 256], BF16, st) for _ in range(NB)]
        Wd = [self.sb([128, 2, D], BF16, st) for _ in range(NB)]
        bWg = [Buf("wg%d" % i) for i in range(NB)]
        bWu = [Buf("wu%d" % i) for i in range(NB)]
        bWd = [Buf("wd%d" % i) for i in range(NB)]
        A = [self.sb([128, 2, NT], BF16, st) for _ in range(2)]
        bA = [[Buf("a%d_%d" % (i, ti)) for ti in range(3)] for i in range(2)]
        pg, pu, pd = self.psr([0, 1]), self.psr([2, 3]), self.psr([4, 5, 6, 7])
        NG = ng if ng is not None else DFF // 256

        def load(fg):
            sl = fg % NB
            k.dma("pool", Wg[sl][:], wg[:, fg * 256:(fg + 1) * 256].rearrange("(k p) n -> p k n", p=128), writes=[bWg[sl]])
            k.dma("pool", Wu[sl][:], wu[:, fg * 256:(fg + 1) * 256].rearrange("(k p) n -> p k n", p=128), writes=[bWu[sl]])
            k.dma("pool", Wd[sl][:], wd[fg * 256:(fg + 1) * 256, :].rearrange("(f p) n -> p f n", p=128), writes=[bWd[sl]])

        def gate_up(fg, ti):
            sl = fg % NB
            a, ba = A[fg % 2], bA[fg % 2]
            c0, c1, s = tiles[ti]
            w = c1 - c0
            for fc in range(2):
                g_ps, bg = pg.next()
                k.mm(g_ps[:, :w], bg, [(Wg[sl][:, kk, fc * 128:(fc + 1) * 128], hT[:, kk, c0:c1]) for kk in range(DC)],
                     [bWg[sl], self.bH[ti]])
                u_ps, bu = pu.next()
                k.mm(u_ps[:, :w], bu, [(Wu[sl][:, kk, fc * 128:(fc + 1) * 128], hT[:, kk, c0:c1]) for kk in range(DC)],
                     [bWu[sl], self.bH[ti]])
                sg, bsg = self.scr.next()
                k.op("act", R.activation(out=sg[:, :w], in_=g_ps[:, :w], func=AF.Silu), reads=[bg], writes=[bsg])
                k.op("dve", R.tensor_tensor(out=a[:, fc, c0:c1], in0=sg[:, :w], in1=u_ps[:, :w], op=ALU.mult),
                     reads=[bsg, bu], writes=[ba[ti]])

        def down(fg, ti):
            sl = fg % NB
            a, ba = A[fg % 2], bA[fg % 2]
            c0, c1, s = tiles[ti]
            w = c1 - c0
            for dc in range(DC):
                d_ps, bd = pd.next()
                k.mm(d_ps[:, :w], bd, [(Wd[sl][:, fc, dc * 128:(dc + 1) * 128], a[:, fc, c0:c1]) for fc in range(2)],
                     [bWd[sl], ba[ti]])
                k.op("dve", R.scalar_tensor_tensor(
                    out=xT[:, dc, c0:c1], in0=d_ps[:, :w], scalar=self.m_gate(l, n, dc, s), in1=xT[:, dc, c0:c1],
                    op0=ALU.mult, op1=ALU.add), reads=[bd, self.bM, self.bX[dc][ti]], writes=[self.bX[dc][ti]])

        load(0)
        pending = None
        for fg in range(NG):
            for ti in range(len(tiles)):
                gate_up(fg, ti)
                if pending is not None:
                    down(*pending)
                if ti == 0 and fg + 1 < NG:
                    load(fg + 1)
                pending = (fg, ti)
        down(*pending)
        self.end_phase(st)

    def alloc_pw(self, st):
        if getattr(st, "_pw", None) is None:
            st._pw = ([self.sb([128, DC, 256], BF16, st) for _ in range(2)], [Buf("pw%d" % i) for i in range(2)])
        return st._pw

    def proj_fm(self, w_ap, col0, nchunks, tiles, handler, st, src=None, bsrc=None):
        k = self.k
        src = src if src is not None else self.hT
        bsrc = bsrc if bsrc is not None else self.bH
        NB = 2
        W, bW = self.alloc_pw(st)
        pm = self.psr([0, 1, 2, 3])
        ng = (nchunks + 1) // 2

        def load(g):
            cw = min(2, nchunks - g * 2) * 128
            k.dma("pool", W[g % NB][:, :, 0:cw], w_ap[:, col0 + g * 256:col0 + g * 256 + cw].rearrange("(k p) n -> p k n", p=128),
                  writes=[bW[g % NB]])
        load(0)
        for g in range(ng):
            if g + 1 < ng:
                load(g + 1)
            sl = g % NB
            for ci in range(min(2, nchunks - g * 2)):
                ch = g * 2 + ci
                for ti, (c0, c1, s) in enumerate(tiles):
                    w = c1 - c0
                    ps, bps = pm.next()
                    k.mm(ps[:, :w], bps, [(W[sl][:, kk, ci * 128:(ci + 1) * 128], src[:, kk, c0:c1]) for kk in range(DC)],
                         [bW[sl], bsrc[ti]])
                    handler(ch, ti, (c0, c1, s), ps, bps)

    def proj_tm(self, w_ap, col0, ncols, v_out, st, nchunks_tok=9):
        k = self.k
        hT = self.hT
        NB = 2
        W, bW = self.alloc_pw(st)
        if getattr(st, "_vo", None) is None:
            st._vo = Rot([(self.sb([128, 256], BF16, st), Buf("vo%d" % i)) for i in range(3)])
        vo = st._vo
        pm = self.psr([4, 5, 6, 7])
        ng = ncols // 256

        def load(g):
            k.dma("pool", W[g % NB][:], w_ap[:, col0 + g * 256:col0 + (g + 1) * 256].rearrange("(k p) n -> p k n", p=128),
                  writes=[bW[g % NB]])
        load(0)
        for g in range(ng):
            if g + 1 < ng:
                load(g + 1)
            sl = g % NB
            for tc in range(nchunks_tok):
                tok0 = tc * 128
                tw = 128 if tc < 8 else NCX
                ti = min(tc // 4, 2)
                ps, bps = pm.next()
                k.mm(ps[:tw, 0:256], bps, [(hT[:, kk, tok0:tok0 + tw], W[sl][:, kk, :]) for kk in range(DC)],
                     [bW[sl], self.bH[ti]])
                t, bt = vo.next()
                k.op("act", R.copy(t[:tw, :], ps[:tw, 0:256]), reads=[bps], writes=[bt])
                k.dma("sp", v_out[tok0:tok0 + tw, g * 256:(g + 1) * 256], t[:tw, :], reads=[bt], sem_of=bt)

    def qk_post(self, ps, bps, tile, gain_ap, rope, out_t, b_out, cos=None, sin=None, ssr=None):
        k = self.k
        c0, c1, s = tile
        w = c1 - c0
        ones = self.ones
        qf, bqf = self.scr.next()
        sq, bsq = self.scr.next()
        k.op("act", R.copy(qf[:, :w], ps[:, :w]), reads=[bps], writes=[bqf])
        k.op("act", R.activation(out=sq[:, :w], in_=ps[:, :w], func=AF.Square), reads=[bps], writes=[bsq])
        ss, bss = ssr.next()
        k.op("pe", R.matmul(ss[:, :w], lhsT=ones[:], rhs=sq[:, :w], start=True, stop=True),
             reads=[bsq, self.bC], writes=[bss])
        r, br = self.scr.next()
        k.op("dve", R.tensor_scalar(out=r[:, :w], in0=ss[:, :w], scalar1=1.0 / 128, scalar2=EPS,
                                              op0=ALU.mult, op1=ALU.add), reads=[bss], writes=[br])
        k.op("act", R.activation(out=r[:, :w], in_=r[:, :w], func=AF.Sqrt), reads=[br], writes=[br])
        k.op("dve", R.reciprocal(out=r[:, :w], in_=r[:, :w]), reads=[br], writes=[br])
        if rope and s == 0:
            qn, bqn = self.scr.next()
            k.op("dve", R.scalar_tensor_tensor(out=qn[:, :w], in0=qf[:, :w], scalar=gain_ap, in1=r[:, :w],
                                                         op0=ALU.mult, op1=ALU.mult), reads=[bqf, br, self.bG], writes=[bqn])
            t1, bt1 = self.scr.next()
            t2, bt2 = self.scr.next()
            k.op("pool", R.tensor_tensor(out=t1[:, :w], in0=qn[:, :w], in1=cos[:, c0:c1], op=ALU.mult),
                 reads=[bqn, self.bR], writes=[bt1])
            k.op("pool", R.tensor_tensor(out=t2[0:64, :w], in0=qn[64:128, :w], in1=sin[64:128, c0:c1], op=ALU.mult),
                 reads=[bqn, self.bR], writes=[bt2])
            k.op("pool", R.tensor_tensor(out=t2[64:128, :w], in0=qn[0:64, :w], in1=sin[0:64, c0:c1], op=ALU.mult),
                 reads=[bqn, self.bR], writes=[bt2])
            k.op("dve", R.tensor_tensor(out=out_t[0:64, c0:c1], in0=t1[0:64, :w], in1=t2[0:64, :w], op=ALU.subtract),
                 reads=[bt1, bt2], writes=[b_out])
            k.op("dve", R.tensor_tensor(out=out_t[64:128, c0:c1], in0=t1[64:128, :w], in1=t2[64:128, :w], op=ALU.add),
                 reads=[bt1, bt2], writes=[b_out])
        else:
            k.op("dve", R.scalar_tensor_tensor(out=out_t[:, c0:c1], in0=qf[:, :w], scalar=gain_ap, in1=r[:, :w],
                                                         op0=ALU.mult, op1=ALU.mult), reads=[bqf, br, self.bG], writes=[b_out])

    def load_gains(self, gains, st):
        k = self.k
        self.gn = self.sb([128, len(gains)], F32, st)
        self.bG = Buf("gains")
        for i, g in enumerate(gains):
            k.dma("sp", self.gn[:, i:i + 1], g.rearrange("(p o) -> p o", o=1), writes=[self.bG])

    def load_rope(self, cos_ap, sin_ap, st):
        k = self.k
        self.cos = self.sb([128, NL], F32, st)
        self.sin = self.sb([128, NL], F32, st)
        self.bR = Buf("rope")
        k.dma("sp", self.cos[:], cos_ap, writes=[self.bR])
        k.dma("sp", self.sin[:], sin_ap, writes=[self.bR])

    def proj_l0(self, w_in, gains, cos_ap, sin_ap, outs):
        k = self.k
        st = ExitStack()
        self.load_gains(gains, st)
        self.load_rope(cos_ap, sin_ap, st)
        qo = Rot([(self.sb([128, NT], BF16, st), Buf("qo%d" % i)) for i in range(2)])
        ssr = self.psr([4, 5])
        groups = [("qa", 0, 8, 0, False), ("qb", 8, 8, 2, True), ("ka", 16, 8, 1, False), ("kb", 32, 2, 3, True)]
        for nm, ch0, nch, gi, rope in groups:
            cur = {}

            def handler(ch, ti, tile, ps, bps, nm=nm, gi=gi, rope=rope, cur=cur):
                if ti == 0:
                    cur["t"] = qo.next()
                t, bt = cur["t"]
                self.qk_post(ps, bps, tile, self.gn[:, gi:gi + 1], rope, t, bt, self.cos, self.sin, ssr)
                if ti == len(TILES) - 1:
                    k.dma("sp", outs[nm][ch, :, :], t[:], reads=[bt], sem_of=bt)
            self.proj_fm(w_in, ch0 * 128, nch, TILES, handler, st)
        self.proj_tm(w_in, 24 * 128, 1024, outs["va"], st)
        self.proj_tm(w_in, 34 * 128, 256, outs["vb"], st)
        self.end_phase(st)

    def proj_l1(self, w_in, gains, cos_ap, sin_ap, outs):
        k = self.k
        st = ExitStack()
        self.load_gains(gains, st)
        self.load_rope(cos_ap, sin_ap, st)
        qo = Rot([(self.sb([128, NT], BF16, st), Buf("qo%d" % i)) for i in range(2)])
        ssr = self.psr([4, 5])
        cur = {}

        def hq(ch, ti, tile, ps, bps):
            if ti == 0:
                cur["t"] = qo.next()
            t, bt = cur["t"]
            self.qk_post(ps, bps, tile, self.gn[:, 0:1], True, t, bt, self.cos, self.sin, ssr)
            if ti == 1:
                k.dma("sp", outs["q"][ch, :, :], t[:, 0:NL], reads=[bt], sem_of=bt)

        def hk(ch, ti, tile, ps, bps):
            if ti == 0:
                cur["t"] = qo.next()
            t, bt = cur["t"]
            self.qk_post(ps, bps, tile, self.gn[:, 1:2], True, t, bt, self.cos, self.sin, ssr)
            if ti == 2:
                k.dma("sp", outs["k"][ch, :, :], t[:], reads=[bt], sem_of=bt)
        self.proj_fm(w_in, 0, 16, TILES[:2], hq, st)
        self.proj_fm(w_in, 16 * 128, 4, TILES, hk, st)
        self.proj_tm(w_in, 20 * 128, 512, outs["v"], st)
        self.end_phase(st)

    def out_proj(self, w_out, l, tiles):
        k = self.k
        st = ExitStack()
        xT = self.xT

        def handler(ch, ti, tile, ps, bps):
            c0, c1, s = tile
            w = c1 - c0
            k.op("dve", R.scalar_tensor_tensor(
                out=xT[:, ch, c0:c1], in0=ps[:, :w], scalar=self.m_gate(l, 1, ch, s), in1=xT[:, ch, c0:c1],
                op0=ALU.mult, op1=ALU.add), reads=[bps, self.bM, self.bX[ch][ti]], writes=[self.bX[ch][ti]])
        self.proj_fm(w_out, 0, 16, tiles, handler, st)
        self.end_phase(st)

    def attn_l0(self, qa, qb, ka, kb, va, vb, nab, swm_ap, sink_ap, fused=None):
        k = self.k
        st = ExitStack()
        oT = self.hT
        onesb = self.onesb
        kT = [(self.sb([128, NWIN * 128], BF16, st), Buf("kT%d" % i)) for i in range(2)]
        vv = [(self.sb([128, NWIN, 128], BF16, st), Buf("vv%d" % i)) for i in range(2)]
        qq = [(self.sb([128, 4, NT], BF16, st), Buf("qq%d" % i)) for i in range(2)]
        bm = Rot([(self.sb([128, 6, 128], F32, st), Buf("bm%d" % i)) for i in range(2)])
        swm = self.sb([128, 8, 2, 128], F32, st)
        bswm = Buf("swm")
        esink = self.sb([128, 8], F32, st)
        bes = Buf("esink")
        k.dma("sp", swm[:], swm_ap, writes=[bswm])
        k.dma("sp", esink[:], sink_ap.rearrange("(o n) -> o n", o=1).partition_broadcast(128), writes=[bes])
        k.op("act", R.activation(out=esink[:], in_=esink[:], func=AF.Exp), reads=[bes], writes=[bes])
        pS = self.psr([0, 1, 2, 3])
        pO = self.psr([4, 5])
        pD = self.psr([6, 7])
        bO = self.bH
        if fused is not None:
            shK, gK, shV, gV, selp_ap, seln_ap, bG = fused
            gK3 = gK.rearrange("(r x) t -> x r t", r=NCORES)
            gV3 = gV.rearrange("(r t) c -> t r c", r=NCORES)
            sel = self.sb([128, 2, NCORES], F32, st)
            bsel = Buf("sel")
            k.dma("sp", sel[:, 0, :], selp_ap, writes=[bsel])
            k.dma("sp", sel[:, 1, :], seln_ap, writes=[bsel])
            candK = Rot([(self.sb([128, NCORES, 256], BF16, st), Buf("ck%d" % i)) for i in range(2)])
            candV = Rot([(self.sb([128, NCORES, 2, 128], BF16, st), Buf("cv%d" % i)) for i in range(2)])

            def select(dst, cand, bcand, side, bdst):
                k.op("dve", R.tensor_scalar(out=dst, in0=cand(0), scalar1=sel[:, side, 0:1], scalar2=None, op0=ALU.mult),
                     reads=[bcand, bsel], writes=[bdst])
                for r in range(1, NCORES):
                    k.op("dve", R.scalar_tensor_tensor(out=dst, in0=cand(r), scalar=sel[:, side, r:r + 1], in1=dst,
                                                        op0=ALU.mult, op1=ALU.add), reads=[bcand, bsel], writes=[bdst])

            def load_k(hk, kt, bkt):
                k.dma("sp", kt[:, 256:1280], shK[hk, :, 0:NL], writes=[bkt], nowaw=True)
                k.dma("sp", kt[:, 1536:1792].rearrange("p (r t) -> p r t", r=NCORES),
                      gK3[hk * 128:(hk + 1) * 128, :, NL:NT], reads=[bG], writes=[bkt], nowaw=True)
                for side, (c0, c1, d0) in enumerate(((768, 1024, 0), (0, 256, 1280))):
                    ct, bct = candK.next()
                    k.dma("sp", ct[:], gK3[hk * 128:(hk + 1) * 128, :, c0:c1], reads=[bG], writes=[bct], nowaw=True)
                    select(kt[:, d0:d0 + 256], (lambda r, ct=ct: ct[:, r, :]), bct, side, bkt)

            def load_v(hv, vt, bvt):
                cols = slice(hv * 128, (hv + 1) * 128)
                k.dma("sp", vt[:, 2:10, :], shV[0:NL, cols].rearrange("(j p) d -> p j d", p=128), writes=[bvt], nowaw=True)
                for r in range(NCORES):
                    k.dma("sp", vt[(r % 4) * 32:(r % 4 + 1) * 32, 12 + r // 4, :], gV[r * NT + NL:(r + 1) * NT, cols],
                          reads=[bG], writes=[bvt], nowaw=True)
                for side, (t0, w0) in enumerate(((768, 0), (0, 10))):
                    ct, bct = candV.next()
                    for j in range(2):
                        k.dma("sp", ct[:, :, j, :], gV3[t0 + j * 128:t0 + (j + 1) * 128, :, cols], reads=[bG], writes=[bct], nowaw=True)
                    select(vt[:, w0:w0 + 2, :], (lambda r, ct=ct: ct[:, r, :, :]), bct, side, bvt)
        else:
            def load_k(hk, kt, bkt):
                src = ka[hk, :, :] if hk < 8 else kb[hk - 8, :, :]
                k.dma("sp", kt[:], src, writes=[bkt])

            def load_v(hv, vt, bvt):
                src = va[hv, :, :, :] if hv < 8 else vb[hv - 8, :, :, :]
                k.dma("sp", vt[:], src, writes=[bvt])

        def finalize(o_ps, bo, d_ps, bd, w3, out_ap, ti, sink_heads=None):
            rd, brd = self.scr.next()
            if sink_heads is None:
                k.op("dve", R.reciprocal(out=rd[:, :w3], in_=d_ps[:, :w3]), reads=[bd], writes=[brd])
            else:
                wq = w3 // 4
                for j, h in enumerate(sink_heads):
                    k.op("dve", R.tensor_scalar(
                        out=rd[:, j * wq:(j + 1) * wq], in0=d_ps[:, j * wq:(j + 1) * wq], scalar1=esink[:, h:h + 1],
                        scalar2=None, op0=ALU.add), reads=[bd, bes], writes=[brd])
                k.op("dve", R.reciprocal(out=rd[:, :w3], in_=rd[:, :w3]), reads=[brd], writes=[brd])
            if sink_heads is None:
                k.op("dve", R.tensor_tensor(out=out_ap, in0=o_ps[:, :w3], in1=rd[:, :w3], op=ALU.mult),
                     reads=[bo, brd], writes=[bO[ti]])
            else:
                wq = w3 // 4
                k.op("dve", R.tensor_tensor(
                    out=out_ap, in0=o_ps[:, :w3].rearrange("p (h q) -> p h q", h=4),
                    in1=rd[:, :w3].rearrange("p (h q) -> p h q", h=4), op=ALU.mult),
                    reads=[bo, brd], writes=[bO[ti]])

        for h in range(8):
            kt, bkt = kT[h % 2]
            vt, bvt = vv[h % 2]
            qt, bqt = qq[h % 2]
            load_k(h, kt, bkt)
            load_v(h, vt, bvt)
            k.dma("sp", qt[:, 0, :], qa[h, :, :], writes=[bqt])
            for jt in range(9):
                ti = min(jt // 4, 2)
                q0 = jt * 128
                qw = 128 if jt < 8 else NCX
                groups = []
                if jt < 8:
                    w0 = min(jt, 6)
                    bt_, bbt = bm.next()
                    k.dma("sp", bt_[:], nab[jt, h, :, :, :], writes=[bbt])
                    groups.append(([w0, w0 + 1, w0 + 2], bt_[:, 0:3, :], bbt))
                    groups.append(([w0 + 3, w0 + 4, w0 + 5], bt_[:, 3:6, :], bbt))
                groups.append(([12, 13], None, None))
                o_ps, bo = pO.next()
                d_ps, bd = pD.next()
                nmm = sum(len(g[0]) for g in groups)
                im = 0
                for wl, bias, bbias in groups:
                    n = len(wl)
                    s_ps, bs = pS.next()
                    for i, wi in enumerate(wl):
                        k.op("pe", R.matmul(
                            s_ps[:, i * qw:(i + 1) * qw], lhsT=kt[:, wi * 128:(wi + 1) * 128], rhs=qt[:, 0, q0:q0 + qw],
                            start=True, stop=True), reads=[bkt, bqt], writes=[bs], signal=(i == n - 1))
                    p_t, bp = self.hb.next()
                    if bias is not None:
                        t, bt = self.scr.next()
                        k.op("dve", R.scalar_tensor_tensor(
                            out=t[:, :n * qw].rearrange("p (s q) -> p s q", s=n), in0=s_ps[:, :n * qw].rearrange("p (s q) -> p s q", s=n),
                            scalar=SCALE, in1=bias, op0=ALU.mult, op1=ALU.add), reads=[bs, bbias], writes=[bt])
                        k.op("act", R.activation(out=p_t[:, :n * qw], in_=t[:, :n * qw], func=AF.Exp),
                             reads=[bt], writes=[bp])
                    else:
                        k.op("act", R.activation(out=p_t[:, :n * qw], in_=s_ps[:, :n * qw],
                                                                                   func=AF.Exp, scale=SCALE),
                             reads=[bs], writes=[bp])
                    for i, wi in enumerate(wl):
                        first = (im == 0)
                        last = (im == nmm - 1)
                        k.op("pe", R.matmul(
                            o_ps[:, :qw], lhsT=vt[:, wi, :], rhs=p_t[:, i * qw:(i + 1) * qw], start=first, stop=last),
                            reads=[bvt, bp], writes=[bo], signal=last)
                        k.op("pe", R.matmul(
                            d_ps[:, :qw], lhsT=onesb[:], rhs=p_t[:, i * qw:(i + 1) * qw], start=first, stop=last),
                            reads=[bp, self.bC], writes=[bd], signal=(last or i == n - 1))
                        im += 1
                finalize(o_ps, bo, d_ps, bd, qw, oT[:, h, q0:q0 + qw], ti)

        for kv in range(2):
            kt, bkt = kT[kv % 2]
            vt, bvt = vv[kv % 2]
            qt, bqt = qq[kv % 2]
            load_k(8 + kv, kt, bkt)
            load_v(8 + kv, vt, bvt)
            for j in range(4):
                k.dma("sp", qt[:, j, :], qb[kv * 4 + j, :, :], writes=[bqt])
            for jt in range(9):
                ti = min(jt // 4, 2)
                q0 = jt * 128
                qw = 128 if jt < 8 else NCX
                steps = []
                if jt < 8:
                    steps.append((jt + 1, 0))
                    steps.append((jt + 2, None))
                    steps.append((jt + 3, 1))
                steps.append((12, None))
                steps.append((13, None))
                o_ps, bo = pO.next()
                d_ps, bd = pD.next()
                w4 = 4 * qw
                for si, (wi, mside) in enumerate(steps):
                    s_ps, bs = pS.next()
                    k.op("pe", R.matmul(
                        s_ps[:, :w4].rearrange("p (h q) -> p h q", h=4), lhsT=kt[:, wi * 128:(wi + 1) * 128],
                        rhs=qt[:, :, q0:q0 + qw], start=True, stop=True), reads=[bkt, bqt], writes=[bs])
                    p_t, bp = self.hb.next()
                    if mside is not None:
                        t, bt = self.scr.next()
                        for j in range(4):
                            k.op("dve", R.scalar_tensor_tensor(
                                out=t[:, j * qw:(j + 1) * qw], in0=s_ps[:, j * qw:(j + 1) * qw], scalar=SCALE,
                                in1=swm[:, jt, mside, :], op0=ALU.mult, op1=ALU.add), reads=[bs, bswm], writes=[bt])
                        k.op("act", R.activation(out=p_t[:, :w4], in_=t[:, :w4], func=AF.Exp),
                             reads=[bt], writes=[bp])
                    else:
                        k.op("act", R.activation(out=p_t[:, :w4], in_=s_ps[:, :w4], func=AF.Exp,
                                                                              scale=SCALE), reads=[bs], writes=[bp])
                    first = (si == 0)
                    last = (si == len(steps) - 1)
                    k.op("pe", R.matmul(
                        o_ps[:, :w4], lhsT=vt[:, wi, :], rhs=p_t[:, :w4], start=first, stop=last),
                        reads=[bvt, bp], writes=[bo], signal=last)
                    k.op("pe", R.matmul(
                        d_ps[:, :w4], lhsT=onesb[:], rhs=p_t[:, :w4], start=first, stop=last),
                        reads=[bp, self.bC], writes=[bd], signal=True)
                finalize(o_ps, bo, d_ps, bd, w4, oT[:, 8 + kv * 4:8 + kv * 4 + 4, q0:q0 + qw], ti,
                         sink_heads=[kv * 4 + j for j in range(4)])
        self.end_phase(st)

    def attn_l1(self, q_ap, k_ap, v_ap, fused=None):
        k = self.k
        st = ExitStack()
        oT = self.hT
        onesb = self.onesb
        NK = 66
        kT = [(self.sb([128, NK * 128], BF16, st), Buf("kT%d" % i)) for i in range(2)]
        vv = [(self.sb([128, NK, 128], BF16, st), Buf("vv%d" % i)) for i in range(2)]
        qq = Rot([(self.sb([128, NL], BF16, st), Buf("qq%d" % i)) for i in range(2)])
        pS = self.psr([0, 1, 2, 7])
        pO = self.psr([3, 4])
        pD = self.psr([5, 6])
        accs = [(self.sb([128, 512], F32, st), Buf("accD")), (self.sb([128, 512], F32, st), Buf("accP"))]
        for kv in range(4):
            kt, bkt = kT[kv % 2]
            vt, bvt = vv[kv % 2]
            if fused is None:
                for part in range(2):
                    k.dma("sp", kt[:, part * 33 * 128:(part + 1) * 33 * 128], k_ap[kv, :, part * 33 * 128:(part + 1) * 33 * 128],
                          writes=[bkt])
                for part in range(2):
                    k.dma("sp", vt[:, part * 33:(part + 1) * 33, :], v_ap[kv, :, part * 33:(part + 1) * 33, :], writes=[bvt])
            else:
                gK, gV, bG = fused
                gK3 = gK.rearrange("(r x) t -> x r t", r=NCORES)
                for half in range(2):
                    k.dma("sp", kt[:, half * 4096:(half + 1) * 4096].rearrange("p (r t) -> p r t", r=4),
                          gK3[kv * 128:(kv + 1) * 128, half * 4:(half + 1) * 4, 0:NL], reads=[bG], writes=[bkt], nowaw=True)
                k.dma("sp", kt[:, 8192:8448].rearrange("p (r t) -> p r t", r=NCORES),
                      gK3[kv * 128:(kv + 1) * 128, :, NL:NT], reads=[bG], writes=[bkt], nowaw=True)
                cols = slice(kv * 128, (kv + 1) * 128)
                for r in range(NCORES):
                    k.dma("sp", vt[:, r * 8:(r + 1) * 8, :], gV[r * NT:r * NT + NL, cols].rearrange("(j p) d -> p j d", p=128),
                          reads=[bG], writes=[bvt], nowaw=True)
                    k.dma("sp", vt[(r % 4) * 32:(r % 4 + 1) * 32, 64 + r // 4, :], gV[r * NT + NL:(r + 1) * NT, cols],
                          reads=[bG], writes=[bvt], nowaw=True)
            for hq in range(4):
                qt, bqt = qq.next()
                k.dma("sp", qt[:], q_ap[kv * 4 + hq, :, :], writes=[bqt])
                for qt2 in range(2):
                    q0 = qt2 * 512
                    rhs_q = qt[:, q0:q0 + 512]
                    o_ps, bo = pO.next()
                    d_ps, bd = pD.next()
                    pend = None

                    def s_step(kc):
                        s_ps, bs = pS.next()
                        k.op("pe", R.matmul(s_ps[:, :], lhsT=kt[:, kc * 128:(kc + 1) * 128], rhs=rhs_q,
                                                      start=True, stop=True), reads=[bkt, bqt], writes=[bs])
                        p_t, bp = self.hb.next()
                        k.op("act", R.activation(out=p_t[:, :], in_=s_ps[:, :], func=AF.Exp, scale=SCALE),
                             reads=[bs], writes=[bp])
                        return (kc, p_t, bp)

                    def pv_step(item):
                        kc, p_t, bp = item
                        first = (kc == 0)
                        last = (kc == NK - 1)
                        k.op("pe", R.matmul(o_ps[:, :], lhsT=vt[:, kc, :], rhs=p_t[:, :], start=first, stop=last),
                             reads=[bvt, bp], writes=[bo], signal=True)
                        which = 1 if kc % 3 == 2 else 0
                        acc, bacc = accs[which]
                        eng = "dve"
                        if kc < 3 and kc % 3 in (0, 2):
                            k.op(eng, R.tensor_copy(acc[:, :], p_t[:, :]), reads=[bp], writes=[bacc])
                        else:
                            k.op(eng, R.tensor_tensor(out=acc[:, :], in0=acc[:, :], in1=p_t[:, :], op=ALU.add),
                                 reads=[bp, bacc], writes=[bacc])

                    pend = [s_step(0), s_step(1)]
                    for kc in range(2, NK):
                        pend.append(s_step(kc))
                        pv_step(pend.pop(0))
                    while pend:
                        pv_step(pend.pop(0))
                    k.op("pe", R.matmul(d_ps[:, :], lhsT=self.ones[:], rhs=accs[0][0][:, :], start=True, stop=False),
                         reads=[accs[0][1], self.bC], writes=[bd], signal=True)
                    k.op("pe", R.matmul(d_ps[:, :], lhsT=self.ones[:], rhs=accs[1][0][:, :], start=False, stop=True),
                         reads=[accs[1][1], self.bC], writes=[bd], signal=True)
                    rd, brd = self.scr.next()
                    k.op("dve", R.reciprocal(out=rd[:, :], in_=d_ps[:, :]), reads=[bd], writes=[brd])
                    k.op("dve", R.tensor_tensor(
                        out=oT[:, kv * 4 + hq, q0:q0 + 512], in0=o_ps[:, :], in1=rd[:, :], op=ALU.mult),
                        reads=[bo, brd], writes=[self.bH[qt2]])
        self.end_phase(st)


def build_A():
    p = P()
    k = p.k
    aw = p.din("aw", [2, D, 2304])
    ab = p.din("ab", [2, 2304])
    cc = p.din("cc", [2, D])
    modo = p.dout("modo", [128, 2, 18, 2])
    p.consts()
    st = ExitStack()
    bsc = Buf("sc")
    scf = p.sb([128, 32], F32, st)
    p.load_cols(cc.rearrange("s (k p) -> (s k) p", p=128), 32, scf[:], bsc, st)
    k.op("act", R.activation(out=scf[:], in_=scf[:], func=AF.Silu), reads=[bsc], writes=[bsc])
    bT = p.sb([128, 36], F32, st)
    bbT = Buf("bT")
    p.load_cols(ab.rearrange("l (c p) -> (l c) p", p=128), 36, bT[:], bbT, st)
    mo = p.sb([128, 2, 18, 2], F32, st)
    bmo = Buf("mo")
    sc3 = scf[:].rearrange("p (s k) -> p s k", s=2)
    W = Rot([(p.sb([128, DC, 384], F32, st), Buf("aw%d" % i)) for i in range(4)])
    for l in range(2):
        ps, bps = p.PS[1 + l], p.bPS[1 + l]
        for i in range(6):
            wt, bw = W.next()
            k.dma("sp", wt[:], aw[l, :, i * 384:(i + 1) * 384].rearrange("(k p) n -> p k n", p=128), writes=[bw])
            for c3 in range(3):
                ch = i * 3 + c3
                k.mm(ps[:, ch * 2:(ch + 1) * 2], bps, [(wt[:, kk, c3 * 128:(c3 + 1) * 128], sc3[:, :, kk]) for kk in range(DC)],
                     [bw, bsc])
        for s in range(2):
            k.op("dve", R.tensor_tensor(
                out=mo[:, l, :, s], in0=ps[:, 0:36].rearrange("p (c s) -> p c s", s=2)[:, :, s], in1=bT[:, l * 18:(l + 1) * 18],
                op=ALU.add), reads=[bps, bbT], writes=[bmo])
    k.dma("sp", modo, mo[:], reads=[bmo], sem_of=bmo)
    p.end_phase(st)
    return p.finish()


def _common_inputs(p):
    mod = p.din("mod", [128, 2, 144, 2])
    normg = p.din("normg", [2, 3, D])
    return mod, normg


def build_B():
    p = P()
    x_in = p.din("x_in", [NT, D])
    mod, normg = _common_inputs(p)
    wg = p.din("wg", [D, DFF]); wu = p.din("wu", [D, DFF]); wd = p.din("wd", [DFF, D])
    w_in = p.din("w_in", [D, 4608])
    gains = [p.din(n, [128]) for n in ("g_naq", "g_nak", "g_swq", "g_swk")]
    cos = p.din("cos", [128, NL]); sin = p.din("sin", [128, NL])
    xT_o = p.dout("xT_o", [128, DC, NT])
    outs = {"qa": p.dout("qa", [8, 128, NT], BF16), "qb": p.dout("qb", [8, 128, NT], BF16),
            "ka": p.dout("ka", [8, 128, NT], BF16), "kb": p.dout("kb", [2, 128, NT], BF16),
            "va": p.dout("va", [NT, 1024], BF16), "vb": p.dout("vb", [NT, 256], BF16)}
    p.consts()
    p.state()
    p.load_mod(mod, normg, [0])
    p.load_x_tokmajor(x_in)
    p.norm_mod(0, 0, TILES)
    p.ffn(0, 0, TILES, wg, wu, wd)
    p.norm_mod(0, 1, TILES)
    p.store_xT(xT_o)
    p.proj_l0(w_in, gains, cos, sin, outs)
    return p.finish()


def build_C():
    p = P()
    xT_i = p.din("xT_i", [128, DC, NT])
    mod, normg = _common_inputs(p)
    qa = p.din("qa", [8, 128, NT], BF16); qb = p.din("qb", [8, 128, NT], BF16)
    ka = p.din("kaw", [8, 128, NWIN * 128], BF16); kb = p.din("kbw", [2, 128, NWIN * 128], BF16)
    va = p.din("vaw", [8, 128, NWIN, 128], BF16); vb = p.din("vbw", [2, 128, NWIN, 128], BF16)
    nab = p.din("nab", [8, 8, 128, 6, 128]); swm = p.din("swm", [128, 8, 2, 128]); sink = p.din("sink", [8])
    w_out = p.din("w_out", [D, D])
    wg0 = p.din("wg0", [D, DFF]); wu0 = p.din("wu0", [D, DFF]); wd0 = p.din("wd0", [DFF, D])
    wg1 = p.din("wg1", [D, DFF]); wu1 = p.din("wu1", [D, DFF]); wd1 = p.din("wd1", [DFF, D])
    w_in = p.din("w_in", [D, 3072])
    gains = [p.din(n, [128]) for n in ("g_q", "g_k")]
    cos = p.din("cos", [128, NL]); sin = p.din("sin", [128, NL])
    xT_o = p.dout("xT_o", [128, DC, NT])
    outs = {"q": p.dout("q", [16, 128, NL], BF16), "k": p.dout("k", [4, 128, NT], BF16), "v": p.dout("v", [NT, 512], BF16)}
    p.consts()
    p.state()
    p.load_mod(mod, normg, [0, 1])
    p.load_xT(xT_i)
    p.attn_l0(qa, qb, ka, kb, va, vb, nab, swm, sink)
    p.out_proj(w_out, 0, TILES)
    p.norm_mod(0, 2, TILES)
    p.ffn(0, 2, TILES, wg0, wu0, wd0)
    p.norm_mod(1, 0, TILES)
    p.ffn(1, 0, TILES, wg1, wu1, wd1)
    p.norm_mod(1, 1, TILES)
    p.store_xT(xT_o)
    p.proj_l1(w_in, gains, cos, sin, outs)
    return p.finish()


def build_D():
    p = P()
    xT_i = p.din("xT_i", [128, DC, NT])
    mod, normg = _common_inputs(p)
    q = p.din("q", [16, 128, NL], BF16)
    kk = p.din("kall", [4, 128, 66 * 128], BF16)
    vv = p.din("vall", [4, 128, 66, 128], BF16)
    w_out = p.din("w_out", [D, D])
    wg = p.din("wg", [D, DFF]); wu = p.din("wu", [D, DFF]); wd = p.din("wd", [DFF, D])
    out = p.dout("out", [NL, D])
    p.consts()
    p.state()
    p.load_mod(mod, normg, [1])
    p.load_xT(xT_i)
    p.attn_l1(q, kk, vv)
    p.out_proj(w_out, 1, TILES[:2])
    p.norm_mod(1, 2, TILES[:2])
    p.ffn(1, 2, TILES[:2], wg, wu, wd)
    p.store_x_tokmajor(out)
    return p.finish()


def build_F():
    p = P()
    nc = p.nc
    aw = p.din("aw", [2, D, 2304]); ab = p.din("ab", [2, 2304]); cc = p.din("cc", [2, D])
    x_in = p.din("x_in", [NT, D])
    normg = p.din("normg", [2, 3, D])
    W = {}
    for l in range(2):
        for w_ in range(2):
            W[(l, w_)] = (p.din("wg%d%d" % (l, w_), [D, DFF]), p.din("wu%d%d" % (l, w_), [D, DFF]), p.din("wd%d%d" % (l, w_), [DFF, D]))
    ab_in = p.din("ab_in", [D, 4608]); ab_out = p.din("ab_out", [D, D])
    g_in = p.din("g_in", [D, 3072]); g_out = p.din("g_out", [D, D])
    gains0 = [p.din(n, [128]) for n in ("g_naq", "g_nak", "g_swq", "g_swk")]
    gains1 = [p.din(n, [128]) for n in ("g_q", "g_k")]
    cos = p.din("cos", [128, NL]); sin = p.din("sin", [128, NL])
    nab = p.din("nab", [8, 8, 128, 6, 128]); swm = p.din("swm", [128, 8, 2, 128]); sink = p.din("sink", [8])
    selp = p.din("selp", [128, NCORES]); seln = p.din("seln", [128, NCORES])
    out = p.dout("out", [NL, D])
    dt_ = lambda name, shape, dt: nc.dram_tensor(name, list(shape), dt).ap()
    sh_mod = dt_("sh_mod", [128, 72], F32); g_mod = dt_("g_mod", [NCORES * 128, 72], F32)
    qa_d = dt_("qa_d", [8, 128, NT], BF16); qb_d = dt_("qb_d", [8, 128, NT], BF16)
    shK0 = dt_("shK0", [10, 128, NT], BF16); gK0 = dt_("gK0", [NCORES * 1280, NT], BF16)
    shV0 = dt_("shV0", [NT, 1280], BF16); gV0 = dt_("gV0", [NCORES * NT, 1280], BF16)
    q1_d = dt_("q1_d", [16, 128, NL], BF16)
    shK1 = dt_("shK1", [4, 128, NT], BF16); gK1 = dt_("gK1", [NCORES * 512, NT], BF16)
    shV1 = dt_("shV1", [NT, 512], BF16); gV1 = dt_("gV1", [NCORES * NT, 512], BF16)
    bGm, bG0, bG1 = Buf("gmod"), Buf("g0"), Buf("g1")

    p.consts()
    p.state()
    p.adaln(aw, ab, cc, sh_mod)
    p.allgather(sh_mod, g_mod, bGm)
    p.load_mod(None, normg, [0, 1], gm=(g_mod, bGm))
    p.load_x_tokmajor(x_in)
    p.norm_mod(0, 0, TILES)
    p.ffn(0, 0, TILES, *W[(0, 0)])
    p.norm_mod(0, 1, TILES)
    p.proj_l0(ab_in, gains0, cos, sin, {"qa": qa_d, "qb": qb_d, "ka": shK0[0:8], "kb": shK0[8:10],
                                        "va": shV0[:, 0:1024], "vb": shV0[:, 1024:1280]})
    p.allgather(shK0.rearrange("h d t -> (h d) t"), gK0, bG0)
    p.allgather(shV0, gV0, bG0)
    p.attn_l0(qa_d, qb_d, None, None, None, None, nab, swm, sink, fused=(shK0, gK0, shV0, gV0, selp, seln, bG0))
    p.out_proj(ab_out, 0, TILES)
    p.norm_mod(0, 2, TILES)
    p.ffn(0, 2, TILES, *W[(0, 1)])
    p.norm_mod(1, 0, TILES)
    p.ffn(1, 0, TILES, *W[(1, 0)])
    p.norm_mod(1, 1, TILES)
    p.proj_l1(g_in, gains1, cos, sin, {"q": q1_d, "k": shK1, "v": shV1})
    p.allgather(shK1.rearrange("h d t -> (h d) t"), gK1, bG1)
    p.allgather(shV1, gV1, bG1)
    p.attn_l1(q1_d, None, None, fused=(gK1, gV1, bG1))
    p.out_proj(g_out, 1, TILES[:2])
    p.norm_mod(1, 2, TILES[:2])
    p.ffn(1, 2, TILES[:2], *W[(1, 1)])
    p.store_x_tokmajor(out)
    return p.finish()


_PROGS = {}


def _prog(name):
    if name not in _PROGS:
        _PROGS[name] = {"A": build_A, "B": build_B, "C": build_C, "D": build_D, "F": build_F}[name]()
    return _PROGS[name]


def _run(name, in_maps):
    res = run_bass_kernel_spmd(_prog(name), in_maps, core_ids=list(range(NCORES)))
    return res.results


def _rope_tables():
    t = np.arange(SEQ)
    row = (t // GRID_W).astype(np.float32)
    col = (t % GRID_W).astype(np.float32)
    inv = (np.float32(10000.0) ** (-np.arange(0, 64, 2, dtype=np.float32) / np.float32(64))).astype(np.float32)
    ang = np.concatenate([row[:, None] * inv, col[:, None] * inv], axis=-1).astype(np.float32)
    cos = np.cos(ang).astype(np.float32)
    sin = np.sin(ang).astype(np.float32)
    cosT = np.concatenate([cos, cos], axis=1).T
    sinT = np.concatenate([sin, sin], axis=1).T
    return np.ascontiguousarray(cosT), np.ascontiguousarray(sinT)


def _na_bias(rel_bias, c):
    out = np.full((8, 8, 128, 6, 128), NEG, np.float32)
    kl = np.arange(128)
    ql = np.arange(128)
    for jt in range(8):
        j = 8 * c + jt
        qr = 2 * j + ql // 64
        qc = ql % 64
        rs = np.clip(qr - 4, 0, 120)
        cs = np.clip(qc - 8, 0, 48)
        w0 = min(jt, 6)
        for s in range(6):
            gch = 8 * c - 2 + w0 + s
            if gch < 0 or gch > 63:
                continue
            kr = 2 * gch + kl // 64
            kc = kl % 64
            valid = ((kr[:, None] >= rs[None, :]) & (kr[:, None] < rs[None, :] + 8) &
                     (kc[:, None] >= cs[None, :]) & (kc[:, None] < cs[None, :] + 16))
            dr = np.clip(kr[:, None] - qr[None, :] + 7, 0, 14)
            dc = np.clip(kc[:, None] - qc[None, :] + 15, 0, 30)
            g = rel_bias[:, dr, dc]
            out[jt, :, :, s, :] = np.where(valid[None], g, np.float32(NEG))
    return out


def _sw_mask(c):
    out = np.zeros((128, 8, 2, 128), np.float32)
    kl = np.arange(128)[:, None]
    ql = np.arange(128)[None, :]
    for jt in range(8):
        j = 8 * c + jt
        m0 = np.where(kl >= ql, 0.0, NEG).astype(np.float32)
        m1 = np.where(kl <= ql, 0.0, NEG).astype(np.float32)
        if j - 1 < 0:
            m0[:] = NEG
        if j + 1 > 63:
            m1[:] = NEG
        out[:, jt, 0, :] = m0
        out[:, jt, 1, :] = m1
    return out


def _windows(k_cores, heads):
    lat = np.concatenate([kc[:, :, :NL] for kc in k_cores], axis=2)
    ctx = np.concatenate([kc[:, :, NL:] for kc in k_cores], axis=2)
    res = []
    for c in range(NCORES):
        w = np.zeros((heads, 128, NWIN * 128), lat.dtype)
        lo = (8 * c - 2) * 128
        hi = lo + 12 * 128
        slo, shi = max(lo, 0), min(hi, SEQ)
        w[:, :, slo - lo:shi - lo] = lat[:, :, slo:shi]
        w[:, :, 12 * 128:] = ctx
        res.append(w)
    return res


def _vwindows(v_cores, heads):
    lat = np.concatenate([v[:NL] for v in v_cores], axis=0)
    ctx = np.concatenate([v[NL:] for v in v_cores], axis=0)
    res = []
    for c in range(NCORES):
        w = np.zeros((NWIN * 128, heads * 128), lat.dtype)
        lo = (8 * c - 2) * 128
        hi = lo + 12 * 128
        slo, shi = max(lo, 0), min(hi, SEQ)
        w[slo - lo:shi - lo] = lat[slo:shi]
        w[12 * 128:] = ctx
        w = w.reshape(NWIN, 128, heads, 128).transpose(2, 1, 0, 3)
        res.append(np.ascontiguousarray(w))
    return res


FUSED = False


def kernel(**inputs):
    return kernel_fused(**inputs) if FUSED else kernel_unfused(**inputs)


def kernel_fused(x, c, ctx, c_ctx, adaln_w, adaln_b, norm_g, ffn_w_gate, ffn_w_up, ffn_w_down,
           ab_w_in, ab_w_out, na_q_gain, na_k_gain, na_rel_bias, sw_q_gain, sw_k_gain, sw_sink,
           gqa_w_in, gqa_w_out, gqa_q_gain, gqa_k_gain):
    f = lambda a: np.ascontiguousarray(np.asarray(a, dtype=np.float32))
    x = f(x); ctx = f(ctx); c = f(c); c_ctx = f(c_ctx)
    adaln_w = np.asarray(adaln_w, dtype=np.float32); adaln_b = f(adaln_b); norm_g = f(norm_g)
    cosT, sinT = _rope_tables()
    cc = np.ascontiguousarray(np.stack([c[0], c_ctx], axis=0))
    shared = {"cc": cc, "normg": norm_g, "ab_in": f(ab_w_in[0]), "ab_out": f(ab_w_out[0]), "g_in": f(gqa_w_in[0]),
              "g_out": f(gqa_w_out[0]), "g_naq": f(na_q_gain[0]), "g_nak": f(na_k_gain[0]), "g_swq": f(sw_q_gain[0]),
              "g_swk": f(sw_k_gain[0]), "g_q": f(gqa_q_gain[0]), "g_k": f(gqa_k_gain[0]), "sink": f(sw_sink[0])}
    for l in range(2):
        for w_ in range(2):
            shared["wg%d%d" % (l, w_)] = f(np.asarray(ffn_w_gate)[l, w_])
            shared["wu%d%d" % (l, w_)] = f(np.asarray(ffn_w_up)[l, w_])
            shared["wd%d%d" % (l, w_)] = f(np.asarray(ffn_w_down)[l, w_])
    rel = f(na_rel_bias[0])
    ins = []
    for i in range(NCORES):
        m = dict(shared)
        m["aw"] = np.ascontiguousarray(adaln_w[:, :, i * 2304:(i + 1) * 2304])
        m["ab"] = np.ascontiguousarray(adaln_b[:, i * 2304:(i + 1) * 2304])
        m["x_in"] = np.ascontiguousarray(np.concatenate([x[0, i * NL:(i + 1) * NL], ctx[0, i * NCX:(i + 1) * NCX]], axis=0))
        m["cos"] = np.ascontiguousarray(cosT[:, i * NL:(i + 1) * NL])
        m["sin"] = np.ascontiguousarray(sinT[:, i * NL:(i + 1) * NL])
        m["nab"] = _na_bias(rel, i)
        m["swm"] = _sw_mask(i)
        sp = np.zeros((128, NCORES), np.float32)
        sn = np.zeros((128, NCORES), np.float32)
        if i > 0:
            sp[:, i - 1] = 1.0
        if i < NCORES - 1:
            sn[:, i + 1] = 1.0
        m["selp"] = sp
        m["seln"] = sn
        ins.append(m)
    rF = _run("F", ins)
    out = np.concatenate([r["out"] for r in rF], axis=0)[None]
    return np.ascontiguousarray(out.astype(np.float32))


def kernel_unfused(x, c, ctx, c_ctx, adaln_w, adaln_b, norm_g, ffn_w_gate, ffn_w_up, ffn_w_down,
           ab_w_in, ab_w_out, na_q_gain, na_k_gain, na_rel_bias, sw_q_gain, sw_k_gain, sw_sink,
           gqa_w_in, gqa_w_out, gqa_q_gain, gqa_k_gain):
    f = lambda a: np.ascontiguousarray(np.asarray(a, dtype=np.float32))
    x = f(x); ctx = f(ctx); c = f(c); c_ctx = f(c_ctx)
    adaln_w = np.asarray(adaln_w, dtype=np.float32); adaln_b = f(adaln_b); norm_g = f(norm_g)
    ffn_w_gate = np.asarray(ffn_w_gate, dtype=np.float32); ffn_w_up = np.asarray(ffn_w_up, dtype=np.float32)
    ffn_w_down = np.asarray(ffn_w_down, dtype=np.float32)
    cosT, sinT = _rope_tables()

    cc = np.ascontiguousarray(np.stack([c[0], c_ctx], axis=0))
    ins = []
    for i in range(NCORES):
        ins.append({"aw": np.ascontiguousarray(adaln_w[:, :, i * 2304:(i + 1) * 2304]),
                    "ab": np.ascontiguousarray(adaln_b[:, i * 2304:(i + 1) * 2304]), "cc": cc})
    rA = _run("A", ins)
    mod = np.ascontiguousarray(np.concatenate([r["modo"] for r in rA], axis=2))

    wg00, wu00, wd00 = f(ffn_w_gate[0, 0]), f(ffn_w_up[0, 0]), f(ffn_w_down[0, 0])
    abin = f(ab_w_in[0])
    ins = []
    for i in range(NCORES):
        x_in = np.ascontiguousarray(np.concatenate([x[0, i * NL:(i + 1) * NL], ctx[0, i * NCX:(i + 1) * NCX]], axis=0))
        ins.append({"x_in": x_in, "mod": mod, "normg": norm_g, "wg": wg00, "wu": wu00, "wd": wd00, "w_in": abin,
                    "g_naq": f(na_q_gain[0]), "g_nak": f(na_k_gain[0]), "g_swq": f(sw_q_gain[0]), "g_swk": f(sw_k_gain[0]),
                    "cos": np.ascontiguousarray(cosT[:, i * NL:(i + 1) * NL]), "sin": np.ascontiguousarray(sinT[:, i * NL:(i + 1) * NL])})
    rB = _run("B", ins)
    del wg00, wu00, wd00

    kaw = _windows([r["ka"] for r in rB], 8)
    kbw = _windows([r["kb"] for r in rB], 2)
    vaw = _vwindows([r["va"] for r in rB], 8)
    vbw = _vwindows([r["vb"] for r in rB], 2)
    rel = f(na_rel_bias[0])
    wg01, wu01, wd01 = f(ffn_w_gate[0, 1]), f(ffn_w_up[0, 1]), f(ffn_w_down[0, 1])
    wg10, wu10, wd10 = f(ffn_w_gate[1, 0]), f(ffn_w_up[1, 0]), f(ffn_w_down[1, 0])
    about = f(ab_w_out[0]); gin = f(gqa_w_in[0])
    ins = []
    for i in range(NCORES):
        ins.append({"xT_i": rB[i]["xT_o"], "mod": mod, "normg": norm_g, "qa": rB[i]["qa"], "qb": rB[i]["qb"],
                    "kaw": kaw[i], "kbw": kbw[i], "vaw": vaw[i], "vbw": vbw[i],
                    "nab": _na_bias(rel, i), "swm": _sw_mask(i), "sink": f(sw_sink[0]),
                    "w_out": about, "wg0": wg01, "wu0": wu01, "wd0": wd01, "wg1": wg10, "wu1": wu10, "wd1": wd10,
                    "w_in": gin, "g_q": f(gqa_q_gain[0]), "g_k": f(gqa_k_gain[0]),
                    "cos": np.ascontiguousarray(cosT[:, i * NL:(i + 1) * NL]), "sin": np.ascontiguousarray(sinT[:, i * NL:(i + 1) * NL])})
    rC = _run("C", ins)
    del rB, kaw, kbw, vaw, vbw, wg01, wu01, wd01, wg10, wu10, wd10

    k_lat = np.concatenate([r["k"][:, :, :NL] for r in rC], axis=2)
    k_ctx = np.concatenate([r["k"][:, :, NL:] for r in rC], axis=2)
    kall = np.ascontiguousarray(np.concatenate([k_lat, k_ctx], axis=2))
    v_all = np.concatenate([r["v"][:NL] for r in rC] + [r["v"][NL:] for r in rC], axis=0)
    vall = np.ascontiguousarray(v_all.reshape(66, 128, 4, 128).transpose(2, 1, 0, 3))
    wg11, wu11, wd11 = f(ffn_w_gate[1, 1]), f(ffn_w_up[1, 1]), f(ffn_w_down[1, 1])
    gout = f(gqa_w_out[0])
    ins = []
    for i in range(NCORES):
        ins.append({"xT_i": rC[i]["xT_o"], "mod": mod, "normg": norm_g, "q": rC[i]["q"], "kall": kall, "vall": vall,
                    "w_out": gout, "wg": wg11, "wu": wu11, "wd": wd11})
    rD = _run("D", ins)
    out = np.concatenate([r["out"] for r in rD], axis=0)[None]
    return np.ascontiguousarray(out.astype(np.float32))
```
